# Optimizing a Trainium2 kernel written in Bass

```python
import math
import jax, jax.numpy as jnp
from jax import lax
import numpy as np

D_MODEL = 1024
BATCH = 16
SEQ = 2048
DEPTH = 4

CTX_LEN = 256
GRID_W = 64
D_MIX = D_MODEL
D_S5 = D_MIX // 2
D_LRU = D_MIX - D_S5
S5_GROUP_CH = 16
S5_GROUPS = D_S5 // S5_GROUP_CH
S5_STATE = 64
LRU_HEADS = 8
LRU_HEAD_DIM = D_LRU // LRU_HEADS
LRU_CONV = 4
LRU_PAD = (2, 1)
LRU_C = 8.0
D_FF = 2816
FFN_CONV = 3
N_DIR = 2
N_MOD = 6
LN_EPS = 1e-6
DEEPNORM_ALPHA = (2 * DEPTH) ** 0.25
DEEPNORM_BETA = (8 * DEPTH) ** -0.25

kernel_name = "hybrid_s5_rglru_convffn_deepnorm_dit"


def layer_norm(x, g=None, b=None):
    xf = x.astype(jnp.float32)
    mu = jnp.mean(xf, axis=-1, keepdims=True)
    var = jnp.mean(jnp.square(xf - mu), axis=-1, keepdims=True)
    y = ((xf - mu) * lax.rsqrt(var + LN_EPS)).astype(x.dtype)
    if g is not None:
        y = y * g + b
    return y


def ada_params(cond, w_mod, b_mod):
    m = jax.nn.silu(cond) @ w_mod + b_mod
    return jnp.split(m[..., None, :], N_MOD, axis=-1)


def modulate(x, shift, scale):
    return layer_norm(x) * (1.0 + scale) + shift


def linear_scan(a, b, h0, reverse):
    if h0 is not None:
        idx = -1 if reverse else 0
        b = b.at[:, idx].add(a[:, idx] * h0)

    def combine(left, right):
        return left[0] * right[0], right[0] * left[1] + right[1]

    _, h = lax.associative_scan(combine, (a, b), reverse=reverse, axis=1)
    return h


def dwconv_seq(u, w, b, pad):
    c = u.shape[-1]
    out = lax.conv_general_dilated(u, w[:, None, :], (1,), (pad,),
                                   dimension_numbers=("NWC", "WIO", "NWC"),
                                   feature_group_count=c)
    return out + b


def dwconv_grid(u, w, b, rows):
    bsz, length, c = u.shape
    img = u.reshape(bsz, rows, GRID_W, c)
    out = lax.conv_general_dilated(img, w[:, :, None, :], (1, 1), ((1, 1), (1, 1)),
                                   dimension_numbers=("NHWC", "HWIO", "NHWC"),
                                   feature_group_count=c)
    return out.reshape(bsz, length, c) + b


def s5_direction(ug, s0, lam_re, lam_im, log_dt, b_re, b_im, reverse):
    lam = lax.complex(lam_re.astype(jnp.float32), lam_im.astype(jnp.float32))
    dt = jnp.exp(log_dt.astype(jnp.float32))[:, None]
    lam_bar = jnp.exp(lam * dt)
    b_mat = lax.complex(b_re.astype(jnp.float32), b_im.astype(jnp.float32))
    b_bar = ((lam_bar - 1.0) / lam)[..., None] * b_mat
    bu = jnp.einsum("blgc,gpc->blgp", ug.astype(jnp.complex64), b_bar)
    return linear_scan(jnp.broadcast_to(lam_bar, bu.shape), bu, s0, reverse)


def s5_readout(states, c_re, c_im):
    c_mat = lax.complex(c_re.astype(jnp.float32), c_im.astype(jnp.float32))
    return jnp.real(jnp.einsum("blgp,gcp->blgc", states, c_mat))


def rglru_direction(xc, h0, w_a, b_a, w_x, b_x, lam, reverse):
    bsz, length, dl = xc.shape
    xh = xc.reshape(bsz, length, LRU_HEADS, LRU_HEAD_DIM)
    r = jax.nn.sigmoid(jnp.einsum("blhi,hij->blhj", xh, w_a).reshape(bsz, length, dl) + b_a)
    i = jax.nn.sigmoid(jnp.einsum("blhi,hij->blhj", xh, w_x).reshape(bsz, length, dl) + b_x)
    log_a = (-LRU_C * r.astype(jnp.float32)) * jax.nn.softplus(-lam.astype(jnp.float32))
    a = jnp.exp(log_a)
    b = jnp.sqrt(-jnp.expm1(2.0 * log_a)) * (i * xc).astype(jnp.float32)
    return linear_scan(a, b, h0, reverse)


def token_mixers(h, inits, lw, want_out, want_finals):
    bsz, length, _ = h.shape
    proj = h @ lw["w_in"]
    u_s5 = proj[..., :D_S5]
    x_lru = proj[..., D_S5:D_S5 + D_LRU]
    g_lru = proj[..., D_S5 + D_LRU:]
    if inits is None:
        inits = (None, None, None, None)
    ug = u_s5.astype(jnp.float32).reshape(bsz, length, S5_GROUPS, S5_GROUP_CH)
    s5_states = [s5_direction(ug, inits[d], lw["s5_lam_re"][d], lw["s5_lam_im"][d], lw["s5_log_dt"][d],
                              lw["s5_b_re"][d], lw["s5_b_im"][d], d == 1) for d in range(N_DIR)]
    xc = dwconv_seq(x_lru, lw["lru_conv_w"], lw["lru_conv_b"], LRU_PAD)
    lru_states = [rglru_direction(xc, inits[2 + d], lw["lru_w_a"][d], lw["lru_b_a"][d], lw["lru_w_x"][d],
                                  lw["lru_b_x"][d], lw["lru_lam"][d], d == 1) for d in range(N_DIR)]
    finals = None
    if want_finals:
        finals = (s5_states[0][:, -1], s5_states[1][:, 0], lru_states[0][:, -1], lru_states[1][:, 0])
    if not want_out:
        return None, finals
    y = (s5_readout(s5_states[0], lw["s5_c_re"][0], lw["s5_c_im"][0])
         + s5_readout(s5_states[1], lw["s5_c_re"][1], lw["s5_c_im"][1])
         + lw["s5_d"].astype(jnp.float32).reshape(S5_GROUPS, S5_GROUP_CH) * ug)
    y = jax.nn.gelu(y.reshape(bsz, length, D_S5))
    y = y * jax.nn.sigmoid(y @ lw["s5_w_glu"] + lw["s5_b_glu"])
    y_s5 = y.astype(h.dtype)
    y_lru = (lru_states[0] + lru_states[1]).astype(h.dtype) * jax.nn.gelu(g_lru)
    out = jnp.concatenate([y_s5, y_lru], axis=-1) @ lw["w_out"]
    return out, finals


def conv_ffn(h, w_up, conv_w, conv_b, w_down, rows):
    u, v = jnp.split(h @ w_up, 2, axis=-1)
    if rows is None:
        u = dwconv_seq(u, conv_w[1], conv_b, (1, 1))
    else:
        u = dwconv_grid(u, conv_w, conv_b, rows)
    return (jax.nn.gelu(u) * v) @ w_down


def setup_inputs(seed: int = 0) -> dict:
    key = jax.random.key(seed)
    ks = iter(jax.random.split(key, 48))
    f32 = jnp.float32

    def nrm(shape, scale):
        return scale * jax.random.normal(next(ks), shape, f32)

    x = nrm((BATCH, SEQ, D_MODEL), 1.0)
    c = nrm((BATCH, D_MODEL), 1.0)
    ctx = nrm((BATCH, CTX_LEN, D_MODEL), 1.0)
    c_ctx = nrm((D_MODEL,), 1.0)
    w_mod = nrm((DEPTH, D_MODEL, N_MOD * D_MODEL), 0.5 * D_MODEL ** -0.5)
    b_mod = nrm((DEPTH, N_MOD * D_MODEL), 0.01)
    w_in = nrm((DEPTH, D_MODEL, D_S5 + 2 * D_LRU), D_MODEL ** -0.5)
    s5_shape = (DEPTH, N_DIR, S5_GROUPS, S5_STATE)
    s5_lam_re = -0.5 + nrm(s5_shape, 0.01)
    s5_lam_im = math.pi * jnp.arange(S5_STATE, dtype=f32) + nrm(s5_shape, 0.01)
    s5_log_dt = jax.random.uniform(next(ks), (DEPTH, N_DIR, S5_GROUPS), f32, math.log(1e-3), math.log(1e-1))
    s5_b_re = nrm((DEPTH, N_DIR, S5_GROUPS, S5_STATE, S5_GROUP_CH), (2 * S5_GROUP_CH) ** -0.5)
    s5_b_im = nrm((DEPTH, N_DIR, S5_GROUPS, S5_STATE, S5_GROUP_CH), (2 * S5_GROUP_CH) ** -0.5)
    s5_c_re = nrm((DEPTH, N_DIR, S5_GROUPS, S5_GROUP_CH, S5_STATE), (2 * S5_STATE) ** -0.5)
    s5_c_im = nrm((DEPTH, N_DIR, S5_GROUPS, S5_GROUP_CH, S5_STATE), (2 * S5_STATE) ** -0.5)
    s5_d = nrm((DEPTH, D_S5), 1.0)
    s5_w_glu = nrm((DEPTH, D_S5, D_S5), D_S5 ** -0.5)
    s5_b_glu = nrm((DEPTH, D_S5), 0.01)
    lru_conv_w = nrm((DEPTH, LRU_CONV, D_LRU), LRU_CONV ** -0.5)
    lru_conv_b = nrm((DEPTH, D_LRU), 0.01)
    blk = (DEPTH, N_DIR, LRU_HEADS, LRU_HEAD_DIM, LRU_HEAD_DIM)
    lru_w_a = nrm(blk, LRU_HEAD_DIM ** -0.5)
    lru_b_a = nrm((DEPTH, N_DIR, D_LRU), 0.01)
    lru_w_x = nrm(blk, LRU_HEAD_DIM ** -0.5)
    lru_b_x = nrm((DEPTH, N_DIR, D_LRU), 0.01)
    a_pow = jax.random.uniform(next(ks), (DEPTH, N_DIR, D_LRU), f32, 0.9, 0.999)
    s = a_pow ** (1.0 / LRU_C)
    lru_lam = jnp.log(s) - jnp.log1p(-s)
    w_out = nrm((DEPTH, D_MIX, D_MODEL), DEEPNORM_BETA * D_MIX ** -0.5)
    ln1_g = 1.0 + nrm((DEPTH, D_MODEL), 0.01)
    ln1_b = nrm((DEPTH, D_MODEL), 0.01)
    ffn_w_up = nrm((DEPTH, D_MODEL, 2 * D_FF), D_MODEL ** -0.5)
    ffn_conv_w = nrm((DEPTH, FFN_CONV, FFN_CONV, D_FF), 1.0 / FFN_CONV)
    ffn_conv_b = nrm((DEPTH, D_FF), 0.01)
    ffn_w_down = nrm((DEPTH, D_FF, D_MODEL), DEEPNORM_BETA * D_FF ** -0.5)
    ln2_g = 1.0 + nrm((DEPTH, D_MODEL), 0.01)
    ln2_b = nrm((DEPTH, D_MODEL), 0.01)
    return {"x": x, "c": c, "ctx": ctx, "c_ctx": c_ctx, "w_mod": w_mod, "b_mod": b_mod, "w_in": w_in,
            "s5_lam_re": s5_lam_re, "s5_lam_im": s5_lam_im, "s5_log_dt": s5_log_dt,
            "s5_b_re": s5_b_re, "s5_b_im": s5_b_im, "s5_c_re": s5_c_re, "s5_c_im": s5_c_im,
            "s5_d": s5_d, "s5_w_glu": s5_w_glu, "s5_b_glu": s5_b_glu,
            "lru_conv_w": lru_conv_w, "lru_conv_b": lru_conv_b, "lru_w_a": lru_w_a, "lru_b_a": lru_b_a,
            "lru_w_x": lru_w_x, "lru_b_x": lru_b_x, "lru_lam": lru_lam, "w_out": w_out,
            "ln1_g": ln1_g, "ln1_b": ln1_b, "ffn_w_up": ffn_w_up, "ffn_conv_w": ffn_conv_w,
            "ffn_conv_b": ffn_conv_b, "ffn_w_down": ffn_w_down, "ln2_g": ln2_g, "ln2_b": ln2_b}


def reference(x, c, ctx, c_ctx, w_mod, b_mod, w_in, s5_lam_re, s5_lam_im, s5_log_dt, s5_b_re, s5_b_im,
              s5_c_re, s5_c_im, s5_d, s5_w_glu, s5_b_glu, lru_conv_w, lru_conv_b, lru_w_a, lru_b_a,
              lru_w_x, lru_b_x, lru_lam, w_out, ln1_g, ln1_b, ffn_w_up, ffn_conv_w, ffn_conv_b,
              ffn_w_down, ln2_g, ln2_b):
    rows = x.shape[1] // GRID_W
    xs = x
    cs = ctx
    for l in range(DEPTH):
        last = l == DEPTH - 1
        lw = {"w_in": w_in[l], "s5_lam_re": s5_lam_re[l], "s5_lam_im": s5_lam_im[l], "s5_log_dt": s5_log_dt[l],
              "s5_b_re": s5_b_re[l], "s5_b_im": s5_b_im[l], "s5_c_re": s5_c_re[l], "s5_c_im": s5_c_im[l],
              "s5_d": s5_d[l], "s5_w_glu": s5_w_glu[l], "s5_b_glu": s5_b_glu[l],
              "lru_conv_w": lru_conv_w[l], "lru_conv_b": lru_conv_b[l], "lru_w_a": lru_w_a[l],
              "lru_b_a": lru_b_a[l], "lru_w_x": lru_w_x[l], "lru_b_x": lru_b_x[l], "lru_lam": lru_lam[l],
              "w_out": w_out[l]}
        sh1, sc1, g1, sh2, sc2, g2 = ada_params(c, w_mod[l], b_mod[l])
        csh1, csc1, cg1, csh2, csc2, cg2 = ada_params(c_ctx, w_mod[l], b_mod[l])

        mix_c, finals = token_mixers(modulate(cs, csh1, csc1), None, lw, want_out=not last, want_finals=True)
        mix_x, _ = token_mixers(modulate(xs, sh1, sc1), finals, lw, want_out=True, want_finals=False)

        xs = layer_norm(DEEPNORM_ALPHA * xs + g1 * mix_x, ln1_g[l], ln1_b[l])
        ffn_x = conv_ffn(modulate(xs, sh2, sc2), ffn_w_up[l], ffn_conv_w[l], ffn_conv_b[l], ffn_w_down[l], rows)
        xs = layer_norm(DEEPNORM_ALPHA * xs + g2 * ffn_x, ln2_g[l], ln2_b[l])

        if not last:
            cs = layer_norm(DEEPNORM_ALPHA * cs + cg1 * mix_c, ln1_g[l], ln1_b[l])
            ffn_c = conv_ffn(modulate(cs, csh2, csc2), ffn_w_up[l], ffn_conv_w[l], ffn_conv_b[l], ffn_w_down[l], None)
            cs = layer_norm(DEEPNORM_ALPHA * cs + cg2 * ffn_c, ln2_g[l], ln2_b[l])
    return xs
```

```python
import math
import os
import numpy as np
SKIP = os.environ.get('K_SKIP', '').split(',')
import concourse.bass as bass
import concourse.mybir as mybir
from concourse.bass_utils import run_bass_kernel_spmd

F32 = mybir.dt.float32
BF16 = mybir.dt.bfloat16
AF = mybir.ActivationFunctionType
ALU = mybir.AluOpType

NCORES = 8
DEPTH = 4
D = 1024
NT = 2304
NCTX = 256
TT = 18
NCH = 288
DFF = 2816
NFT = 22
ALPHA = 8.0 ** 0.25
EPS = 1e-6
NPV = 268
C_ID = 0; C_MF = 128; C_MB = 256; C_BAND = 384; C_EV = 384 + 8 * 240; C_SPM = C_EV + 16; C_SMP = C_SPM + 1
NCONST = C_SMP + 1
ESZ = {F32: 4, BF16: 2}


class Op:
    __slots__ = ("eng", "fn", "deps", "ddeps", "sig", "dma", "key", "semval", "idx")


class Prog:
    EPOCH = 6000

    def __init__(self, nc):
        self.nc = nc
        self.ops = []
        self.recs = {}
        self.dma_keys = {}

    @staticmethod
    def region(ap):
        t = ap.tensor
        name = t.name
        esz = ESZ[ap.dtype]
        pairs = ap.ap
        off = int(ap.offset)
        if str(ap.space) == "DRAM":
            lo = hi = off
            for st, cnt in pairs:
                if st >= 0:
                    hi += st * (cnt - 1)
                else:
                    lo += st * (cnt - 1)
            return (name, 0, 1, lo * esz, (hi + 1) * esz)
        row = 1
        for x in t.shape[1:]:
            row *= x
        p0 = off // row
        lo = hi = off - p0 * row
        pcnt = pairs[0][1]
        for st, cnt in pairs[1:]:
            if st >= 0:
                hi += st * (cnt - 1)
            else:
                lo += st * (cnt - 1)
        return (name, p0, p0 + pcnt, lo * esz, (hi + 1) * esz)

    def add(self, eng, fn, reads=(), writes=(), dma=False, key=None):
        op = Op()
        op.eng = eng; op.fn = fn; op.dma = dma; op.key = key; op.sig = False; op.semval = None
        op.idx = len(self.ops)
        deps = set()
        for ap in reads:
            name, p0, p1, b0, b1 = self.region(ap)
            lst = self.recs.setdefault(name, [])
            for r in lst:
                if r[5] and r[4] != op.idx and r[0] < p1 and p0 < r[1] and r[2] < b1 and b0 < r[3]:
                    deps.add(r[4])
            for r in lst:
                if r[4] == op.idx:
                    if (not r[5]) and r[0] == p0 and r[1] == p1 and r[2] == b0 and r[3] == b1:
                        break
                    continue
                if (not r[5]) and r[0] == p0 and r[1] == p1 and r[2] == b0 and r[3] == b1 \
                        and self.ops[r[4]].eng == eng and not self.ops[r[4]].dma and not dma:
                    r[4] = op.idx
                    break
            else:
                lst.append([p0, p1, b0, b1, op.idx, False])
        for ap in writes:
            name, p0, p1, b0, b1 = self.region(ap)
            lst = self.recs.setdefault(name, [])
            keep = []
            for r in lst:
                ov = r[0] < p1 and p0 < r[1] and r[2] < b1 and b0 < r[3]
                if ov and r[4] != op.idx:
                    deps.add(r[4])
                cov = p0 <= r[0] and r[1] <= p1 and b0 <= r[2] and r[3] <= b1
                if not (ov and cov):
                    keep.append(r)
            keep.append([p0, p1, b0, b1, op.idx, True])
            self.recs[name] = keep
        fd = []
        dd = {}
        for di in deps:
            d = self.ops[di]
            if d.dma:
                dd[d.key] = self.dma_keys[d.key] * 16
            elif dma or d.eng != eng:
                fd.append(di)
            elif eng != "pe":
                fd.append(di)
        op.deps = fd
        op.ddeps = dd
        if dma:
            c = self.dma_keys.get(key, 0) + 1
            self.dma_keys[key] = c
            op.semval = c * 16
        self.ops.append(op)
        return op

    def emit(self):
        nc = self.nc
        ops = self.ops
        for op in ops:
            best = {}
            for di in op.deps:
                d = ops[di]
                if d.eng not in best or best[d.eng] < di:
                    best[d.eng] = di
            op.deps = list(best.values())
            for di in op.deps:
                ops[di].sig = True
        cnt = {}
        sems = {}
        for op in ops:
            if op.dma or not op.sig:
                continue
            c = cnt.get(op.eng, 0)
            ep = c // self.EPOCH
            k = (op.eng, ep)
            if k not in sems:
                sems[k] = nc.alloc_semaphore(f"s_{op.eng}_{ep}")
            op.semval = (sems[k], c - ep * self.EPOCH + 1)
            cnt[op.eng] = c + 1
        dsems = {}
        for key in self.dma_keys:
            dsems[key] = nc.alloc_semaphore(f"d_{key}")
        for op in ops:
            if op.dma:
                op.semval = (dsems[op.key], op.semval)
        self.nsem = len(sems) + len(dsems)
        engs = {"pe": [], "act": [], "dve": [], "pool": [], "sp": []}
        for op in ops:
            engs[op.eng].append(op)

        def run(eobj, lst):
            known = {}
            for op in lst:
                for di in op.deps:
                    sem, val = ops[di].semval
                    if known.get(sem.name, 0) < val:
                        eobj.wait_ge(sem, val)
                        known[sem.name] = val
                for key, val in op.ddeps.items():
                    sem = dsems[key]
                    if known.get(sem.name, 0) < val:
                        eobj.wait_ge(sem, val)
                        known[sem.name] = val
                ins = op.fn(eobj)
                if op.dma:
                    ins.then_inc(op.semval[0], 16)
                elif op.sig:
                    ins.then_inc(op.semval[0], 1)
            return known

        with nc.Block() as block:
            @block.tensor
            def _(e):
                run(e, engs["pe"])

            @block.scalar
            def _(e):
                run(e, engs["act"])

            @block.vector
            def _(e):
                run(e, engs["dve"])

            @block.gpsimd
            def _(e):
                run(e, engs["pool"])

            @block.sync
            def _(e):
                known = run(e, engs["sp"])
                for key, c in self.dma_keys.items():
                    sem = dsems[key]
                    if known.get(sem.name, 0) < c * 16:
                        e.wait_ge(sem, c * 16)


class Builder:
    def __init__(self, nlayers=DEPTH, nseq=2, taps=None, limit=99):
        self.limit = limit
        self.nlayers = nlayers
        self.nseq = nseq
        self.taps = taps or []
        nc = bass.Bass("TRN2", target_bir_lowering=False)
        self.nc = nc
        self.P = Prog(nc)
        self.psn = 0
        self.kpn = 0
        self.dram = {}
        self._declare()
        self._alloc()

    def din(self, name, shape, dt=F32):
        self.dram[name] = self.nc.dram_tensor(name, list(shape), dt, kind="ExternalInput").ap()
        return self.dram[name]

    def dscr(self, name, shape, dt=F32, kind="Internal"):
        self.dram[name] = self.nc.dram_tensor(name, list(shape), dt, kind=kind).ap()
        return self.dram[name]

    def _declare(self):
        L = self.nlayers
        self.din("xin", [2, NT, D])
        self.din("condT", [128, 24])
        self.din("w_mod", [L, D, 6 * D])
        self.din("b_mod", [L, 6 * D])
        self.din("w_in_r", [L, 12, 128, 8 * 128])
        self.din("s5pack", [L, 2, 128, 2112])
        self.din("s5dd", [L, 128, 32])
        self.din("log_dt", [L, 2, 32])
        self.din("glu_r", [L, 4, 128, 4 * 128])
        self.din("gate_r", [L, 4, 128, 4 * 128])
        self.din("w_out_r", [L, 8, 128, D])
        self.din("w_up_r", [L, NFT, 128, 8 * 256])
        self.din("w_down_r", [L, NFT, 128, D])
        self.din("pvec", [L, 128, NPV])
        self.din("lnrow", [L, 4, D])
        self.din("consts", [128, NCONST])
        self.dscr("out", [2, 2048, D], kind="ExternalOutput")
        self.dscr("xs_d", [2, NT, D])
        self.dscr("ada_d", [L, 3, 6 * D])
        self.dscr("s5G_d", [L, 32, 128, 128], BF16)
        self.dscr("s5Win_d", [L, 2, 32, 128, 128], BF16)
        self.dscr("s5Mout_d", [L, 2, 2, 64, 32, 128])
        self.dscr("s5mu_d", [L, 2, 2, 64, 32])
        for name, shape, dt in self.taps:
            self.dscr("tap_" + name, shape, dt, kind="ExternalOutput")

    def _alloc(self):
        nc = self.nc
        self.cst = nc.alloc_sbuf_tensor("cst", [128, NCONST], F32)
        self.identb = nc.alloc_sbuf_tensor("identb", [128, 128], BF16)
        self.bandb = nc.alloc_sbuf_tensor("bandb", [128, 8, 240], BF16)
        self.adaT = nc.alloc_sbuf_tensor("adaT", [128, DEPTH, 48, 3], F32)
        self.pv = nc.alloc_sbuf_tensor("pv", [128, 2, NPV], F32)
        self.pv2 = nc.alloc_sbuf_tensor("pv2", [128, 2, 32], F32)
        self.m12 = nc.alloc_sbuf_tensor("m12", [128, 2, 64], F32)
        self.small = nc.alloc_sbuf_tensor("small", [128, 64], F32)
        self.sc3 = nc.alloc_sbuf_tensor("sc3", [128, 3, 64], F32)
        rem = nc.sbuf_bytes_remaining
        self.ASZ = (rem - 2048) // 64 * 64
        self.arena = nc.alloc_sbuf_tensor("arena", [128, self.ASZ // 4], F32)
        self.ps = nc.alloc_psum_tensor("ps", [128, 4096], F32)

    def av(self, off, n, dt=F32):
        assert off % 4 == 0 and off + n * ESZ[dt] <= self.ASZ, (off, n, dt, self.ASZ)
        a = self.arena[:, off // 4:(off + n * ESZ[dt] + 3) // 4]
        if dt == BF16:
            a = a.bitcast(BF16)
            a = a[:, 0:n]
        return a

    def bank(self, n=1):
        if n == 2 and self.psn % 2 == 1:
            self.psn += 1
        b = self.psn % 8
        self.psn += n
        return self.ps[:, b * 512:(b + n) * 512]

    def mm(self, out, lhsT, rhs, start=True, stop=True):
        self.P.add("pe", lambda e: e.matmul(out, lhsT, rhs, start=start, stop=stop),
                   reads=[lhsT, rhs], writes=[out])

    def tr(self, out, in_, ident):
        self.P.add("pe", lambda e: e.transpose(out, in_, ident), reads=[in_, ident], writes=[out])

    def act(self, out, in_, func, bias=0.0, scale=1.0, eng="act"):
        rd = [in_]
        if not isinstance(bias, (int, float)):
            rd.append(bias)
        if not isinstance(scale, (int, float)):
            rd.append(scale)
        self.P.add("act", lambda e: e.activation(out=out, in_=in_, func=func, bias=bias, scale=scale),
                   reads=rd, writes=[out])

    def tt(self, out, in0, in1, op, eng="dve"):
        self.P.add(eng, lambda e: e.tensor_tensor(out, in0, in1, op), reads=[in0, in1], writes=[out])

    def ts(self, out, in0, s1, s2, op0, op1=None, eng="dve"):
        rd = [in0]
        if not isinstance(s1, (int, float)):
            rd.append(s1)
        if s2 is not None and not isinstance(s2, (int, float)):
            rd.append(s2)
        if op1 is None:
            self.P.add(eng, lambda e: e.tensor_scalar(out, in0, s1, None, op0), reads=rd, writes=[out])
        else:
            self.P.add(eng, lambda e: e.tensor_scalar(out, in0, s1, s2, op0, op1), reads=rd, writes=[out])

    def stt(self, out, in0, scalar, in1, op0, op1, eng="dve"):
        rd = [in0, in1]
        if not isinstance(scalar, (int, float)):
            rd.append(scalar)
        self.P.add(eng, lambda e: e.scalar_tensor_tensor(out, in0, scalar, in1, op0, op1), reads=rd, writes=[out])

    def cp(self, out, in_, eng="dve"):
        if eng == "act":
            self.P.add("act", lambda e: e.copy(out, in_), reads=[in_], writes=[out])
        else:
            self.P.add(eng, lambda e: e.tensor_copy(out, in_), reads=[in_], writes=[out])

    def memset(self, out, val, eng="dve"):
        self.P.add(eng, lambda e: e.memset(out, val), reads=[], writes=[out])

    def dma(self, out, in_, key, q="sp", slow=False):
        if slow:
            fn = lambda e: e.dma_start(out=out, in_=in_, allow_slow_non_contiguous=True)
        else:
            fn = lambda e: e.dma_start(out=out, in_=in_)
        self.P.add(q, fn, reads=[in_], writes=[out], dma=True, key=key)

    def tap(self, name, sb_ap):
        if ("tap_" + name) in self.dram:
            self.dma(self.dram["tap_" + name], sb_ap, key="tap")

    def build(self):
        self.load_consts()
        if self.limit >= 1:
            self.prologue_ada()
        if self.limit >= 2:
            for l in range(self.nlayers):
                self.prologue_s5(l)
        if self.limit >= 3:
            for s in range(self.nseq):
                for l in range(self.nlayers):
                    self.layer(s, l)
        if "tap_xs" in self.dram:
            for s_ in range(2):
                for t_ in range(TT):
                    self.dma(self.dram["tap_xs"][s_, t_ * 128:(t_ + 1) * 128, :], self.dram["xs_d"][s_, t_ * 128:(t_ + 1) * 128, :], key="tap")
        self.P.emit()
        return self.nc

    def load_consts(self):
        d = self.dram
        self.dma(self.cst[:], d["consts"], key="cst")
        self.cp(self.identb[:], self.cst[:, C_ID:C_ID + 128])
        self.cp(self.bandb[:].rearrange("p a b -> p (a b)"), self.cst[:, C_BAND:C_BAND + 1920])

    def prologue_ada(self):
        d = self.dram
        A = 0
        condT = self.av(A, 24); A += 96
        siluT = self.av(A, 24, BF16); A += 64
        bm3 = self.av(A, 6 * D); A += 6 * D * 4
        adarow = self.av(A, 6 * D); A += 6 * D * 4
        wm = [self.av(A + i * 8192, 4096, BF16) for i in range(2)]; A += 16384
        self.dma(condT, d["condT"], key="condT")
        self.act(siluT, condT, AF.Silu)
        siluT3 = siluT.rearrange("p (k c) -> p k c", c=3)
        ident = self.cst[:, C_ID:C_ID + 128]
        for l in range(self.nlayers):
            self.dma(bm3[0:3, :], d["b_mod"][l:l + 1, :].partition_broadcast(3).rearrange("p a n -> p (a n)"), key="bm3")
            for cc in range(12):
                w = wm[cc % 2]
                w3 = w.rearrange("p (k n) -> p k n", k=8)
                self.dma(w3, d["w_mod"][l][:, cc * 512:(cc + 1) * 512].rearrange("(k p) n -> p k n", p=128),
                         key=f"wm{cc % 2}", q="pool")
                ps = self.bank()
                for kt in range(8):
                    self.mm(ps[0:3, :], siluT3[:, kt, :], w3[:, kt, :], start=(kt == 0), stop=(kt == 7))
                self.tt(adarow[0:3, cc * 512:(cc + 1) * 512], ps[0:3, :], bm3[0:3, cc * 512:(cc + 1) * 512], ALU.add)
            self.dma(d["ada_d"][l], adarow[0:3, :], key="ada_st")
            ps = self.bank()
            for j in range(48):
                self.tr(ps[:, j * 3:(j + 1) * 3], adarow[0:3, j * 128:(j + 1) * 128], ident[0:3, 0:3])
            aT = self.adaT[:, l].rearrange("p a c -> p (a c)")
            self.cp(aT, ps[:, 0:144])
            for mod in (1, 4):
                v = self.adaT[:, l, mod * 8:(mod + 1) * 8, :].rearrange("p a c -> p (a c)")
                self.ts(v, v, 1.0, None, ALU.add)

    def prologue_s5(self, l):
        d = self.dram
        cst = self.cst
        sgn_pm = cst[:, C_SPM:C_SPM + 1]
        sgn_mp = cst[:, C_SMP:C_SMP + 1]
        EV = cst[:, C_EV:C_EV + 16]
        identf = cst[:, C_ID:C_ID + 128]
        A = [0]

        def al(n, dt=F32):
            v = self.av(A[0], n, dt)
            A[0] += (n * ESZ[dt] + 63) // 64 * 64
            return v

        Q = [al(4096), al(4096)]
        CF = [al(4096), al(4096)]
        SP = al(2112)
        DTb = al(32); dtt = al(32); Aa = al(32); PHI = al(32)
        MAG = al(512); ANG = al(512); TMP = al(512); SN = al(512); CS = al(512); LRe = al(512); LIe = al(512)
        t32 = [al(32) for _ in range(8)]
        P1 = al(512); P2 = al(512); P1n = al(512); P2n = al(512); C2s = al(512); C1pm = al(512); C2n = al(512)
        T1 = al(4096); T2 = al(4096); W = al(4096)
        Wb = al(4096, BF16)
        Gs = al(512); Gs2 = al(512)
        Gb = al(4096, BF16)
        Dd = al(32)
        v3 = lambda x: x.rearrange("p (g c) -> p g c", g=32)
        v4 = lambda x: x.rearrange("p (g s c) -> p g s c", g=32, s=8)
        TWO_PI = 2.0 * math.pi
        for dd in range(2):
            self.dma(SP, d["s5pack"][l, dd], key="s5pack")
            self.dma(DTb, d["log_dt"][l, dd:dd + 1, :].partition_broadcast(128).rearrange("p a n -> p (a n)"), key="s5dt")
            LR = SP[:, 0:32]; LI = SP[:, 32:64]
            B1 = SP[:, 64:576]; B2 = SP[:, 576:1088]; C1 = SP[:, 1088:1600]; C2 = SP[:, 1600:2112]
            self.act(dtt, DTb, AF.Exp)
            self.tt(Aa, LR, dtt, ALU.mult)
            self.tt(PHI, LI, dtt, ALU.mult)
            bc_ge = lambda x: x.unsqueeze(2).to_broadcast([128, 32, 16])
            ev_b = EV.unsqueeze(1).to_broadcast([128, 32, 16])
            self.tt(v3(MAG), bc_ge(Aa), ev_b, ALU.mult)
            self.act(MAG, MAG, AF.Exp)
            self.tt(v3(ANG), bc_ge(PHI), ev_b, ALU.mult)
            MAGIC = 12582912.0
            self.ts(TMP, ANG, 1.0 / TWO_PI, MAGIC, ALU.mult, ALU.add)
            self.ts(TMP, TMP, -MAGIC, None, ALU.add)
            self.stt(TMP, TMP, -TWO_PI, ANG, ALU.mult, ALU.add)
            self.act(SN, TMP, AF.Sin)
            self.ts(ANG, ANG, 0.5 * math.pi, None, ALU.add)
            self.ts(TMP, ANG, 1.0 / TWO_PI, MAGIC, ALU.mult, ALU.add)
            self.ts(TMP, TMP, -MAGIC, None, ALU.add)
            self.stt(TMP, TMP, -TWO_PI, ANG, ALU.mult, ALU.add)
            self.act(CS, TMP, AF.Sin)
            self.tt(LRe, MAG, CS, ALU.mult)
            self.tt(LIe, MAG, SN, ALU.mult)
            LRe3 = v3(LRe); LIe3 = v3(LIe)
            nr, den, rden, kr, ki, u1, u2, krs = t32
            self.ts(nr, LRe3[:, :, 8], -1.0, None, ALU.add)
            l1i = LIe3[:, :, 8]
            self.tt(den, LR, LR, ALU.mult)
            self.tt(u1, LI, LI, ALU.mult)
            self.tt(den, den, u1, ALU.add)
            self.P.add("dve", lambda e, o=rden, i=den: e.reciprocal(o, i), reads=[den], writes=[rden])
            self.tt(u1, nr, LR, ALU.mult)
            self.tt(u2, l1i, LI, ALU.mult)
            self.tt(u1, u1, u2, ALU.add)
            self.tt(kr, u1, rden, ALU.mult)
            self.tt(u1, l1i, LR, ALU.mult)
            self.tt(u2, nr, LI, ALU.mult)
            self.tt(u1, u1, u2, ALU.subtract)
            self.tt(ki, u1, rden, ALU.mult)
            kis = u1
            self.ts(kis, ki, sgn_mp, None, ALU.mult)
            self.ts(krs, kr, sgn_mp, None, ALU.mult)
            self.tt(v3(P1), bc_ge(kr), v3(B1), ALU.mult)
            self.tt(v3(TMP), bc_ge(kis), v3(B2), ALU.mult)
            self.tt(P1, P1, TMP, ALU.add)
            self.tt(v3(P2), bc_ge(krs), v3(B2), ALU.mult)
            self.tt(v3(TMP), bc_ge(ki), v3(B1), ALU.mult)
            self.tt(P2, P2, TMP, ALU.subtract)
            self.ts(P1n, P1, sgn_pm, None, ALU.mult)
            self.ts(P2n, P2, sgn_pm, None, ALU.mult)
            self.ts(C2s, C2, sgn_mp, None, ALU.mult)
            self.ts(C1pm, C1, sgn_pm, None, ALU.mult)
            self.ts(C2n, C2, -1.0, None, ALU.mult)

            def esl(tab3, e0, step):
                if step > 0:
                    return tab3[:, :, e0:e0 + 8]
                stop = e0 - 8
                return tab3[:, :, e0:(stop if stop >= 0 else None):-1]

            def build(dst, e0, step, PA, PB):
                lr = esl(LRe3, e0, step).unsqueeze(3).to_broadcast([128, 32, 8, 16])
                li = esl(LIe3, e0, step).unsqueeze(3).to_broadcast([128, 32, 8, 16])
                pa = v3(PA).unsqueeze(2).to_broadcast([128, 32, 8, 16])
                pb = v3(PB).unsqueeze(2).to_broadcast([128, 32, 8, 16])
                self.tt(v4(T1), lr, pa, ALU.mult)
                self.tt(v4(T2), li, pb, ALU.mult)
                self.tt(dst, T1, T2, ALU.add)

            if dd == 0:
                build(W, 14, -1, P1, P2)
            else:
                build(W, 7, +1, P1, P2)
            W3 = W.rearrange("p (g m) -> p g m", g=32)
            Wb3 = Wb.rearrange("p (g m) -> p g m", g=32)
            for g4 in range(8):
                ps = self.bank()
                for k in range(4):
                    self.tr(ps[:, k * 128:(k + 1) * 128], W3[:, g4 * 4 + k, :], identf)
                self.cp(Wb[:, g4 * 512:(g4 + 1) * 512], ps, eng=("act" if g4 % 2 else "dve"))
            self.dma(d["s5Win_d"][l, dd].rearrange("g p m -> p g m"), Wb3, key="s5st")
            if dd == 0:
                build(Q[dd], 7, -1, P1n, P2n)
                build(CF[dd], 7, +1, C1, C2s)
            else:
                build(Q[dd], 7, +1, P1n, P2n)
                build(CF[dd], 7, -1, C1, C2s)
            if dd == 0:
                build(W, 8, +1, C1pm, C2n)
            else:
                build(W, 15, -1, C1pm, C2n)
            for ri in range(2):
                self.dma(d["s5Mout_d"][l, dd, ri].rearrange("p g m -> p g m"), W3[ri * 64:(ri + 1) * 64, :, :], key="s5st")
            self.dma(d["s5mu_d"][l, dd, 0], LRe3[0:64, :, 15], key="s5st", slow=True)
            self.dma(d["s5mu_d"][l, dd, 1], LIe3[0:64, :, 15], key="s5st", slow=True)
        self.dma(Dd, d["s5dd"][l], key="s5dd")
        maskF = self.cst[:, C_MF:C_MF + 128].unsqueeze(1).to_broadcast([128, 4, 128])
        maskB = self.cst[:, C_MB:C_MB + 128].unsqueeze(1).to_broadcast([128, 4, 128])
        Q3 = [q.rearrange("p (g m) -> p g m", g=32) for q in Q]
        CF3 = [c.rearrange("p (g m) -> p g m", g=32) for c in CF]
        Gb3 = Gb.rearrange("p (g m) -> p g m", g=32)
        for g4 in range(8):
            psF = self.bank(); psB = self.bank()
            for k in range(4):
                g = g4 * 4 + k
                self.mm(psF[:, k * 128:(k + 1) * 128], Q3[0][:, g, :], CF3[0][:, g, :])
                self.mm(psB[:, k * 128:(k + 1) * 128], Q3[1][:, g, :], CF3[1][:, g, :])
            f4 = lambda x: x.rearrange("p (a m) -> p a m", a=4)
            self.tt(f4(Gs), f4(psF), maskF, ALU.mult)
            self.tt(f4(Gs2), f4(psB), maskB, ALU.mult)
            self.tt(Gs, Gs, Gs2, ALU.add)
            for k in range(4):
                g = g4 * 4 + k
                self.stt(Gb3[:, g, :], identf, Dd[:, g:g + 1], Gs[:, k * 128:(k + 1) * 128], ALU.mult, ALU.add)
        self.dma(d["s5G_d"][l].rearrange("g p m -> p g m"), Gb3, key="s5st")

    def ln_stats(self, x, slot, eps=EPS):
        base = slot * 16
        st = self.small[:, base:base + 12].rearrange("p (a b) -> p a b", a=2)
        mv = self.small[:, base + 12:base + 14]
        rstd = self.small[:, base + 14:base + 15]
        nmr = self.small[:, base + 15:base + 16]
        for h in range(2):
            self.P.add("dve", lambda e, o=st[:, h, :], i=x[:, h * 512:(h + 1) * 512]: e.bn_stats(o, i),
                       reads=[x[:, h * 512:(h + 1) * 512]], writes=[st[:, h, :]])
        self.P.add("dve", lambda e, o=mv, i=st: e.bn_aggr(o, i), reads=[st], writes=[mv])
        self.act(rstd, mv[:, 1:2], AF.Sqrt, bias=eps)
        self.P.add("dve", lambda e: e.reciprocal(rstd, rstd), reads=[rstd], writes=[rstd])
        self.stt(nmr, mv[:, 0:1], -1.0, rstd, ALU.mult, ALU.mult)
        return rstd, nmr

    def ln_mod_T(self, s, l, tiles, hT, col0, mod_shift, mod_scale, A0, src_name="xs_d"):
        d = self.dram
        xt = [self.av(A0 + i * 4096, 1024) for i in range(2)]
        xn = [self.av(A0 + 8192 + i * 2048, 1024, BF16) for i in range(4)]
        groups = []
        cur = []
        for t in tiles:
            cond = 2 if t < 2 else s
            if cur and (len(cur) == 4 or cur[0][1] != cond):
                groups.append(cur); cur = []
            cur.append((t, cond))
        if cur:
            groups.append(cur)
        pos = 0
        n = 0
        for grp in groups:
            cond = grp[0][1]
            ng = len(grp)
            for gi, (t, _) in enumerate(grp):
                x = xt[n % 2]; xb = xn[gi]
                self.dma(x, d[src_name][s, t * 128:(t + 1) * 128, :], key=f"lnx{n % 2}")
                rstd, nmr = self.ln_stats(x, n % 4)
                self.act(xb, x, AF.Identity, bias=nmr, scale=rstd)
                n += 1
            for kt in range(8):
                pb = self.bank()
                for gi in range(ng):
                    self.mm(pb[:, gi * 128:(gi + 1) * 128], xn[gi][:, kt * 128:(kt + 1) * 128], self.identb[:])
                src = pb[:, 0:ng * 128]
                dst = hT[:, kt, col0 + pos:col0 + pos + ng * 128]
                sc = self.adaT[:, l, mod_scale * 8 + kt, cond:cond + 1]
                sh = self.adaT[:, l, mod_shift * 8 + kt, cond:cond + 1]
                if kt % 2 == 0:
                    self.act(dst, src, AF.Identity, bias=sh, scale=sc)
                else:
                    self.ts(dst, src, sc, sh, ALU.mult, ALU.add)
            pos += ng * 128

    def layer(self, s, l):
        d = self.dram
        last = (l == DEPTH - 1)
        slot = l % 2
        pv = self.pv[:, slot, :]
        self.dma(pv, d["pvec"][l], key=f"pv{slot}")
        PV_GLUB = 0; PV_LCW = 4; PV_LCB = 20; PV_LBA = 24; PV_LBX = 32; PV_LLAM = 40; PV_FCW = 48; PV_FCB = 246
        coef = self.pv2[:, slot, 0:8]
        hcf = self.pv2[:, slot, 8:16]
        hba = self.pv2[:, slot, 16:24]
        hbx = self.pv2[:, slot, 24:32]
        self.act(coef, pv[:, PV_LLAM:PV_LLAM + 8], AF.Exp, scale=-1.0)
        self.act(coef, coef, AF.Ln, bias=1.0)
        self.ts(coef, coef, -8.0, None, ALU.mult)
        self.ts(hcf, coef, 0.5, None, ALU.mult)
        self.ts(hba, pv[:, PV_LBA:PV_LBA + 8], 0.5, None, ALU.mult)
        self.ts(hbx, pv[:, PV_LBX:PV_LBX + 8], 0.5, None, ALU.mult)

        H1T = 0
        YALL = 36864
        UALL = 73728
        TMPB = 92160
        h1T = self.av(H1T, 8 * NT, BF16).rearrange("p (k t) -> p k t", k=8)
        yall = self.av(YALL, 8 * NT, BF16).rearrange("p (k t) -> p k t", k=8)
        uall = self.av(UALL, 32 * NCH, BF16).rearrange("p (g m) -> p g m", g=32)

        self.ln_mod_T(s, l, list(range(TT)), h1T, 0, mod_shift=0, mod_scale=1, A0=TMPB,
                      src_name=("xin" if l == 0 else "xs_d"))
        if l == 0 and s == 0:
            self.tap("h1T", h1T.rearrange("p k t -> p (k t)"))

        if self.limit < 4:
            return
        chunks = [(0, 512), (512, 512), (1024, 512), (1536, 512), (2048, 256)]

        def load_win(ft, slot_i):
            w = self.av(TMPB + 16384 + slot_i * 2048, 1024, BF16)
            self.dma(w, d["w_in_r"][l, ft], key=f"win{slot_i}", q="pool")
            return w.rearrange("p (k c) -> p k c", k=8)

        ufm = [self.av(TMPB + i * 4608, NT, BF16) for i in range(2)]

        def s5proj(q):
            w_u = load_win(q, 2)
            u = ufm[q % 2]
            for ci, (c0, cn) in enumerate(chunks):
                ps = self.bank()
                for kt in range(8):
                    self.mm(ps[:, 0:cn], w_u[:, kt, :], h1T[:, kt, c0:c0 + cn], start=(kt == 0), stop=(kt == 7))
                self.cp(u[:, c0:c0 + cn], ps[:, 0:cn], eng="dve")
            for g8 in range(8):
                ps = self.bank()
                for j in range(8):
                    self.mm(ps[:, 0:NCH], self.bandb[:, g8, 112 - 16 * j:240 - 16 * j], u[:, j:NT:8],
                            start=(j == 0), stop=(j == 7))
                self.cp(uall[:, q * 8 + g8, :], ps[:, 0:NCH], eng="dve")

        B = TMPB + 16384 + 6144
        xlp = self.av(B, 2310); B += 9280
        xc = self.av(B, NT); B += 9216
        xcb = self.av(B, NT, BF16); B += 4608
        hsum = self.av(B, NT); B += 9216
        gw = self.av(B, 512, BF16); B += 1024
        afull = self.av(B, NT); B += 9216
        bfull = self.av(B, NT); B += 9216
        sfull = self.av(B, NT); B += 9216
        ctmp = [self.av(B + i * 2048, 512) for i in range(4)]; B += 4 * 2048
        gg = [self.av(B + i * 1024, 512, BF16) for i in range(2)]; B += 2048
        assert B <= self.ASZ, (B, self.ASZ)
        XC0 = 2; XL0 = 261
        for q in range(4):
            w_x = load_win(4 + q, 0)
            w_g = load_win(8 + q, 1)
            gw4 = gw.rearrange("p (a c) -> p a c", a=4)
            self.dma(gw, d["gate_r"][l, q], key="gatew", q="pool")
            self.memset(xlp[:, 0:2], 0.0)
            self.memset(xlp[:, 258:261], 0.0)
            self.memset(xlp[:, 2309:2310], 0.0)
            for ci, (c0, cn) in enumerate(chunks):
                ps = self.bank()
                for kt in range(8):
                    self.mm(ps[:, 0:cn], w_x[:, kt, :], h1T[:, kt, c0:c0 + cn], start=(kt == 0), stop=(kt == 7))
                if c0 == 0:
                    self.cp(xlp[:, XC0:XC0 + 256], ps[:, 0:256], eng="act")
                    self.cp(xlp[:, XL0:XL0 + 256], ps[:, 256:512], eng="act")
                else:
                    self.cp(xlp[:, XL0 + c0 - 256:XL0 + c0 - 256 + cn], ps[:, 0:cn], eng="act")
            s5proj(q)
            for (o0, on, i0) in ((0, 256, XC0 - 2), (256, 2048, XL0 - 2)):
                cw = lambda k: pv[:, PV_LCW + q * 4 + k:PV_LCW + q * 4 + k + 1]
                self.ts(xc[:, o0:o0 + on], xlp[:, i0:i0 + on], cw(0), pv[:, PV_LCB + q:PV_LCB + q + 1], ALU.mult, ALU.add)
                for k in range(1, 4):
                    self.stt(xc[:, o0:o0 + on], xlp[:, i0 + k:i0 + k + on], cw(k), xc[:, o0:o0 + on], ALU.mult, ALU.add)
            self.cp(xcb, xc, eng="act")
            if l == 0 and s == 0 and q == 0:
                self.tap("xc0", xc)
            for dd in range(2):
                c_hcf = hcf[:, dd * 4 + q:dd * 4 + q + 1]
                c_hba = hba[:, dd * 4 + q:dd * 4 + q + 1]
                c_hbx = hbx[:, dd * 4 + q:dd * 4 + q + 1]
                for ci, (c0, cn) in enumerate(chunks):
                    psr = self.bank(); psi = self.bank()
                    self.mm(psr[:, 0:cn], gw4[:, dd * 2 + 0, :], xcb[:, c0:c0 + cn])
                    self.mm(psi[:, 0:cn], gw4[:, dd * 2 + 1, :], xcb[:, c0:c0 + cn])
                    t1 = ctmp[(ci % 2) * 2]; t2 = ctmp[(ci % 2) * 2 + 1]
                    self.act(t1[:, 0:cn], psr[:, 0:cn], AF.Tanh, bias=c_hba, scale=0.5)
                    self.act(afull[:, c0:c0 + cn], t1[:, 0:cn], AF.Exp, bias=c_hcf, scale=c_hcf)
                    self.act(t2[:, 0:cn], psi[:, 0:cn], AF.Tanh, bias=c_hbx, scale=0.5)
                    self.stt(bfull[:, c0:c0 + cn], t2[:, 0:cn], 1.0, xc[:, c0:c0 + cn], ALU.add, ALU.mult)
                self.act(sfull, afull, AF.Square)
                self.act(sfull, sfull, AF.Sqrt, bias=1.0, scale=-1.0)
                self.stt(bfull, sfull, 0.5, bfull, ALU.mult, ALU.mult)

                def scan(o, a, b, init, rev):
                    rd = [a, b] + ([] if isinstance(init, float) else [init])
                    if rev:
                        self.P.add("dve", lambda e: e.tensor_tensor_scan(o[:, ::-1], a[:, ::-1], b[:, ::-1], init,
                                                                         ALU.mult, ALU.add), reads=rd, writes=[o])
                    else:
                        self.P.add("dve", lambda e: e.tensor_tensor_scan(o, a, b, init, ALU.mult, ALU.add),
                                   reads=rd, writes=[o])

                if dd == 0:
                    scan(hsum, afull, bfull, 0.0, False)
                else:
                    scan(sfull[:, 0:256], afull[:, 0:256], bfull[:, 0:256], 0.0, True)
                    scan(sfull[:, 256:NT], afull[:, 256:NT], bfull[:, 256:NT], sfull[:, 0:1], True)
                    self.tt(hsum, hsum, sfull, ALU.add)
            for ci, (c0, cn) in enumerate(chunks):
                ps = self.bank()
                for kt in range(8):
                    self.mm(ps[:, 0:cn], w_g[:, kt, :], h1T[:, kt, c0:c0 + cn], start=(kt == 0), stop=(kt == 7))
                g = gg[ci % 2]
                self.act(g[:, 0:cn], ps[:, 0:cn], AF.Gelu_apprx_tanh)
                self.tt(yall[:, 4 + q, c0:c0 + cn], hsum[:, c0:c0 + cn], g[:, 0:cn], ALU.mult)
            if l == 0 and s == 0 and q == 0:
                self.tap("hsum0", hsum)

        if self.limit < 5:
            return
        if self.limit < 6:
            return
        YG = 0
        RING = 18432
        ZS = TMPB
        yg = self.av(YG, 4 * NT, BF16).rearrange("p (k t) -> p k t", k=4)
        zs = self.av(ZS, NCH * 64).rearrange("p (i c) -> p i c", c=64)
        ZE = ZS + NCH * 64 * 4
        ys = [self.av(ZE + i * 4608, 8 * NCH, BF16).rearrange("p (g m) -> p g m", g=8) for i in range(1)]
        assert ZE + 4608 <= self.ASZ
        m1 = self.m12[:, 0, :]; m2 = self.m12[:, 1, :]
        if s == 0 or True:
            mu = d["s5mu_d"][l]
            for gp in range(2):
                for dd in range(2):
                    for ri in range(2):
                        c0 = ri * 32 + dd * 16
                        src_re = mu[dd, 0][:, gp:32:2]
                        src_im = mu[dd, 1][:, gp:32:2]
                        self.dma(m1[gp * 64:(gp + 1) * 64, c0:c0 + 16], src_re, key="mu", slow=True)
                        self.dma(m2[gp * 64:(gp + 1) * 64, c0:c0 + 16], src_im, key="mu", slow=True)
            self.ts(m2[:, 32:64], m2[:, 32:64], -1.0, None, ALU.mult)
        RSZ = 4608
        for g2 in range(16):
            rs = RING + (g2 % 3) * RSZ
            gt = self.av(rs, 256, BF16).rearrange("p (a m) -> p a m", a=2)
            wt = self.av(rs + 512, 512, BF16).rearrange("p (a b m) -> p a b m", a=2, b=2)
            self.dma(wt[:, :, 0, :], d["s5Win_d"][l, 0, 2 * g2:2 * g2 + 2].rearrange("g p m -> p g m"), key=f"s5w{g2 % 3}")
            self.dma(wt[:, :, 1, :], d["s5Win_d"][l, 1, 2 * g2:2 * g2 + 2].rearrange("g p m -> p g m"), key=f"s5w{g2 % 3}")
            for dd in range(2):
                for ri in range(2):
                    ps = self.bank()
                    for gp in range(2):
                        g = 2 * g2 + gp
                        o = ps[gp * 64:(gp + 1) * 64, :]
                        lhs = wt[:, gp, dd, ri * 64:(ri + 1) * 64]
                        if dd == 0:
                            self.mm(o[:, 0:NCH], lhs, uall[:, g, :])
                        else:
                            self.mm(o[:, 0:32], lhs, uall[:, g, 31::-1])
                            self.mm(o[:, 32:NCH], lhs, uall[:, g, NCH - 1:31:-1])
                    col = ri * 32 + dd * 16 + g2
                    self.cp(zs[:, :, col], ps[:, 0:NCH], eng=("act" if (dd + ri) % 2 else "dve"))
        Pt = self.sc3[:, 0:2, :]
        St = self.sc3[:, 2, :]
        mcat = self.m12[:]
        for i in range(1, NCH):
            xx = zs[:, i - 1, :].unsqueeze(1).to_broadcast([128, 2, 64])
            self.tt(Pt, mcat, xx, ALU.mult)
            self.tt(St.rearrange("p (r c) -> p r c", r=2), Pt[:, 0, :].rearrange("p (r c) -> p r c", r=2),
                    Pt[:, 1, :].rearrange("p (r c) -> p r c", r=2)[:, ::-1, :], ALU.add)
            self.tt(zs[:, i, :], zs[:, i, :], St, ALU.add)
        if l == 0 and s == 0:
            self.tap("zs", self.av(ZS, NCH * 64))
        for q in range(4):
            ysq = ys[0]
            for g8 in range(8):
                g = q * 8 + g8
                g2, gp = g // 2, g % 2
                if gp == 0:
                    rs = RING + (g2 % 3) * RSZ
                    gt = self.av(rs, 256, BF16).rearrange("p (a m) -> p a m", a=2)
                    mo = self.av(rs + 512, 512).rearrange("p (a b m) -> p a b m", a=2, b=2)
                    self.dma(gt, d["s5G_d"][l, 2 * g2:2 * g2 + 2].rearrange("g p m -> p g m"), key=f"s5y{g2 % 3}")
                    for pp in range(2):
                        for dd in range(2):
                            for ri in range(2):
                                self.dma(mo[pp * 64:(pp + 1) * 64, dd, ri, :], d["s5Mout_d"][l, dd, ri][:, 2 * g2 + pp, :],
                                         key=f"s5y{g2 % 3}")
                ps = self.bank()
                pr = slice(gp * 64, (gp + 1) * 64)
                self.mm(ps[:, 0:NCH], gt[:, gp, :], uall[:, g, :], start=True, stop=False)
                for ri in range(2):
                    self.mm(ps[:, 1:NCH], mo[pr, 0, ri, :], zs[pr, 0:NCH - 1, ri * 32 + g2], start=False, stop=False)
                for ri in range(2):
                    col = ri * 32 + 16 + g2
                    self.mm(ps[:, 30::-1], mo[pr, 1, ri, :], zs[pr, 0:31, col], start=False, stop=False)
                    self.mm(ps[:, NCH - 1:31:-1], mo[pr, 1, ri, :], zs[pr, 31:NCH - 1, col], start=False, stop=(ri == 1))
                self.cp(ysq[:, g8, :], ps[:, 0:NCH], eng=("act" if g8 % 2 else "dve"))
            for pc in range(5):
                ncw = 64 if pc < 4 else 32
                ps = self.bank()
                for j in range(8):
                    for g8 in range(8):
                        self.mm(ps[:, j:ncw * 8:8], self.bandb[:, j, 112 - 16 * g8:240 - 16 * g8],
                                ysq[:, g8, pc * 64:pc * 64 + ncw], start=(g8 == 0), stop=(g8 == 7))
                self.act(yg[:, q, pc * 512:pc * 512 + ncw * 8], ps[:, 0:ncw * 8], AF.Gelu_apprx_tanh)

        if self.limit < 7:
            return
        if l == 0 and s == 0:
            self.tap("yg", yg.rearrange("p k t -> p (k t)"))
        B = UALL
        wglu = self.av(B, 4 * 4 * 128, BF16).rearrange("p (f k c) -> p f k c", f=4, k=4); B += 4096
        for ft in range(4):
            self.dma(wglu[:, ft].rearrange("p k c -> p (k c)"), d["glu_r"][l, ft], key="wglu", q="pool")
        sg = [self.av(B + i * 1024, 512, BF16) for i in range(2)]; B += 2048
        n = 0
        for ft in range(4):
            for ci, (c0, cn) in enumerate(chunks):
                ps = self.bank()
                for kt in range(4):
                    self.mm(ps[:, 0:cn], wglu[:, ft, kt, :], yg[:, kt, c0:c0 + cn], start=(kt == 0), stop=(kt == 3))
                sgt = sg[n % 2]; n += 1
                self.act(sgt[:, 0:cn], ps[:, 0:cn], AF.Sigmoid, bias=pv[:, PV_GLUB + ft:PV_GLUB + ft + 1])
                self.tt(yall[:, ft, c0:c0 + cn], yg[:, ft, c0:c0 + cn], sgt[:, 0:cn], ALU.mult)
        if l == 0 and s == 0:
            self.tap("yall", yall.rearrange("p k t -> p (k t)"))
        wo = self.av(B, 8 * D, BF16).rearrange("p (k n) -> p k n", k=8); B += 16384
        for kt in range(8):
            self.dma(wo[:, kt, :], d["w_out_r"][l, kt], key="wo", q="pool")
        tiles = list(range(2, TT)) if last else list(range(TT))
        WD = (self.ASZ - NFT * D * 2) // 64 * 64
        wd = self.av(WD, NFT * D, BF16).rearrange("p (f n) -> p f n", f=NFT)
        for ft in range(NFT):
            self.dma(wd[:, ft, :], d["w_down_r"][l, ft], key="wd", q="pool")
        B = self.resid_ln(s, l, tiles, B, mod_gate=2, lnrow0=0, to_out=False, src_name=("xin" if l == 0 else "xs_d"), lim=WD,
                          mmfn=lambda ps2, t: [self.mm(ps2[:, h * 512:(h + 1) * 512], yall[:, kt, t * 128:(t + 1) * 128],
                                                       wo[:, kt, h * 512:(h + 1) * 512], start=(kt == 0), stop=(kt == 7))
                                               for h in range(2) for kt in range(8)])

        if self.limit < 8:
            return
        self.ffn(s, l, wd, WD)

    def resid_ln(self, s, l, tiles, B, mod_gate, lnrow0, to_out, mmfn, tile_off=0, src_name="xs_d", lim=None):
        d = self.dram
        gc = self.av(B, D); B += 4096
        gx = self.av(B, D); B += 4096
        lg = self.av(B, D); B += 4096
        lb = self.av(B, D); B += 4096
        xt = [self.av(B + i * 4096, D) for i in range(3)]; B += 12288
        vt = [self.av(B + i * 4096, D) for i in range(3)]; B += 12288
        assert B <= (self.ASZ if lim is None else lim), (B, self.ASZ, lim)
        ada = d["ada_d"]
        bc = lambda ap: ap.partition_broadcast(128).rearrange("p a n -> p (a n)")
        self.dma(gc, bc(ada[l, 2:3, mod_gate * D:(mod_gate + 1) * D]), key="bc0")
        self.dma(gx, bc(ada[l, s:s + 1, mod_gate * D:(mod_gate + 1) * D]), key="bc1")
        self.dma(lg, bc(d["lnrow"][l, lnrow0:lnrow0 + 1, :]), key="bc2")
        self.dma(lb, bc(d["lnrow"][l, lnrow0 + 1:lnrow0 + 2, :]), key="bc3")
        self.ts(gc, gc, 1.0 / ALPHA, None, ALU.mult)
        self.ts(gx, gx, 1.0 / ALPHA, None, ALU.mult)

        def finish(n, t, x, v):
            self.tt(v, v, lg, ALU.mult)
            self.tt(x, v, lb, ALU.add, eng="pool")
            if to_out:
                if t >= 2:
                    self.dma(d["out"][s, (t - 2) * 128:(t - 1) * 128, :], x, key=f"wx{n % 3}", q="pool")
            else:
                self.dma(d["xs_d"][s, t * 128:(t + 1) * 128, :], x, key=f"wx{n % 3}", q="pool")

        pending = None
        for n, t in enumerate(tiles):
            ps2 = self.bank(2)
            mmfn(ps2, t - tile_off)
            x = xt[n % 3]; v = vt[n % 3]
            self.dma(x, d[src_name][s, t * 128:(t + 1) * 128, :], key=f"rx{n % 3}")
            g = gc if t < 2 else gx
            self.tt(v, ps2, g, ALU.mult)
            self.tt(v, v, x, ALU.add)
            rstd, nmr = self.ln_stats(v, n % 4, eps=EPS / (ALPHA * ALPHA))
            self.act(v, v, AF.Identity, bias=nmr, scale=rstd)
            if pending is not None:
                finish(*pending)
            pending = (n, t, x, v)
        if pending is not None:
            finish(*pending)
        return B

    def ffn(self, s, l, wd, WD):
        d = self.dram
        last = (l == DEPTH - 1)
        slot = l % 2
        pv = self.pv[:, slot, :]
        PV_FCW = 48; PV_FCB = 246
        for ch in range(2):
            if ch == 0:
                ltiles = list(range(0, 10)); own = list(range(0, 9)); r0, r1 = 0, 14
                if last:
                    own = list(range(2, 9)); ltiles = list(range(2, 10))
            else:
                ltiles = list(range(8, 18)); own = list(range(9, 18)); r0, r1 = 14, 32
            t0 = ltiles[0]
            NTK = 1280
            B = 0
            h2T = self.av(B, 8 * NTK, BF16).rearrange("p (k t) -> p k t", k=8); B += 8 * NTK * 2
            hid = self.av(B, NFT * 1152, BF16).rearrange("p (f t) -> p f t", f=NFT); B += NFT * 1152 * 2
            wu = [self.av(B + i * 4096, 2048, BF16).rearrange("p (k c) -> p k c", k=8) for i in range(3)]; B += 12288
            R = r1 - r0
            GP = 66
            ug = self.av(B, (R + 2) * GP, BF16); B += ((R + 2) * GP * 2 + 63) // 64 * 64
            uc = self.av(B, 258, BF16); B += 576
            dgs = [self.av(B + i * 2304, 9 * 128, BF16).rearrange("p (k m) -> p k m", k=9) for i in range(2)]; B += 4608
            gl = self.av(B, 1152, BF16); B += 2304
            TB = B
            self.ln_mod_T(s, l, ltiles, h2T, 0, mod_shift=3, mod_scale=4, A0=TB)
            ug3 = ug.rearrange("p (r c) -> p r c", c=GP)
            self.memset(ug, 0.0)
            self.memset(uc, 0.0)
            loc = lambda r: 256 + 64 * r - t0 * 128
            ra = max(r0 - 1, 0); rb = min(r1 + 1, 32)
            nown = len(own) * 128
            own0 = own[0] * 128 - t0 * 128
            has_ctx = (ch == 0 and not last)
            lo = 256 if has_ctx else 0
            for ft in range(NFT):
                w = wu[ft % 3]
                self.dma(w.rearrange("p k c -> p (k c)"), d["w_up_r"][l, ft], key=f"wu{ft % 3}", q="pool")
                cw = lambda k: pv[:, PV_FCW + ft * 9 + k:PV_FCW + ft * 9 + k + 1]
                cb = pv[:, PV_FCB + ft:PV_FCB + ft + 1]
                dg = dgs[ft % 2]
                for k in range(9):
                    self.ts(dg[:, k, :], self.identb[:], cw(k), None, ALU.mult)
                r = ra
                while r < rb:
                    nr = min(8, rb - r)
                    ps = self.bank()
                    for kt in range(8):
                        self.mm(ps[:, 0:nr * 64], w[:, kt, 0:128], h2T[:, kt, loc(r):loc(r) + nr * 64],
                                start=(kt == 0), stop=(kt == 7))
                    gr = r - (r0 - 1)
                    self.cp(ug3[:, gr:gr + nr, 1:65], ps[:, 0:nr * 64].rearrange("p (r c) -> p r c", c=64), eng="act")
                    r += nr
                rr = 0
                while rr < R:
                    nr = min(8, R - rr)
                    ps = self.bank()
                    po = ps[:, 0:nr * 64].rearrange("p (r c) -> p r c", c=64)
                    for k in range(9):
                        di, dj = k // 3, k % 3
                        self.mm(po, dg[:, k, :], ug3[:, rr + di:rr + di + nr, dj:dj + 64], start=(k == 0), stop=(k == 8))
                    self.act(gl[:, lo + rr * 64:lo + (rr + nr) * 64], ps[:, 0:nr * 64], AF.Gelu_apprx_tanh, bias=cb)
                    rr += nr
                if has_ctx:
                    ps = self.bank()
                    for kt in range(8):
                        self.mm(ps[:, 0:256], w[:, kt, 0:128], h2T[:, kt, 0:256], start=(kt == 0), stop=(kt == 7))
                    self.cp(uc[:, 1:257], ps[:, 0:256], eng="act")
                    ps = self.bank()
                    for k in range(3):
                        self.mm(ps[:, 0:256], dg[:, 3 + k, :], uc[:, k:k + 256], start=(k == 0), stop=(k == 2))
                    self.act(gl[:, 0:256], ps[:, 0:256], AF.Gelu_apprx_tanh, bias=cb)
                c = 0
                while c < nown:
                    cn = min(512, nown - c)
                    ps = self.bank()
                    for kt in range(8):
                        self.mm(ps[:, 0:cn], w[:, kt, 128:256], h2T[:, kt, own0 + c:own0 + c + cn],
                                start=(kt == 0), stop=(kt == 7))
                    self.tt(hid[:, ft, c:c + cn], ps[:, 0:cn], gl[:, c:c + cn], ALU.mult)
                    c += cn
                if l == 0 and s == 0 and ch == 0 and ft == 0:
                    self.tap("hid0", hid[:, 0, :])
            self.resid_ln(s, l, own, TB, mod_gate=5, lnrow0=2, to_out=last, tile_off=own[0], lim=WD,
                          mmfn=lambda ps2, ti: [self.mm(ps2[:, h * 512:(h + 1) * 512], hid[:, ft, ti * 128:(ti + 1) * 128],
                                                        wd[:, ft, h * 512:(h + 1) * 512], start=(ft == 0), stop=(ft == NFT - 1))
                                                for h in range(2) for ft in range(NFT)])


def make_consts():
    c = np.zeros((128, NCONST), np.float32)
    c[:, C_ID:C_ID + 128] = np.eye(128, dtype=np.float32)
    k = np.arange(128)
    s_of = k // 16
    c[:, C_MF:C_MF + 128] = (s_of[None, :] >= s_of[:, None]).astype(np.float32)
    c[:, C_MB:C_MB + 128] = (s_of[:, None] >= s_of[None, :]).astype(np.float32)
    for g8 in range(8):
        band = np.zeros((128, 240), np.float32)
        for kk in range(16 * g8, 16 * g8 + 16):
            band[kk, kk - 16 * g8 + 112] = 1.0
        c[:, C_BAND + g8 * 240:C_BAND + (g8 + 1) * 240] = band
    c[:, C_EV:C_EV + 16] = np.arange(-7, 9, dtype=np.float32)[None, :]
    c[0:64, C_SPM] = 1.0; c[64:128, C_SPM] = -1.0
    c[0:64, C_SMP] = -1.0; c[64:128, C_SMP] = 1.0
    return c


def pack_shared(inp, L=DEPTH):
    f = lambda a: np.ascontiguousarray(np.asarray(a, dtype=np.float32)[:L])
    sh = {}
    sh["w_mod"] = f(inp["w_mod"])
    sh["b_mod"] = f(inp["b_mod"])
    w_in = f(inp["w_in"])
    sh["w_in_r"] = f(w_in.reshape(L, 8, 128, 12, 128).transpose(0, 3, 2, 1, 4).reshape(L, 12, 128, 1024))
    lam_re = f(inp["s5_lam_re"]); lam_im = f(inp["s5_lam_im"])
    b_re = f(inp["s5_b_re"]); b_im = f(inp["s5_b_im"]); c_re = f(inp["s5_c_re"]); c_im = f(inp["s5_c_im"])
    pack = np.zeros((L, 2, 128, 2112), np.float32)
    lrT = lam_re.transpose(0, 1, 3, 2)
    liT = lam_im.transpose(0, 1, 3, 2)
    pack[:, :, 0:64, 0:32] = lrT; pack[:, :, 64:128, 0:32] = lrT
    pack[:, :, 0:64, 32:64] = liT; pack[:, :, 64:128, 32:64] = liT
    brT = b_re.transpose(0, 1, 3, 2, 4).reshape(L, 2, 64, 512)
    biT = b_im.transpose(0, 1, 3, 2, 4).reshape(L, 2, 64, 512)
    crT = c_re.transpose(0, 1, 4, 2, 3).reshape(L, 2, 64, 512)
    ciT = c_im.transpose(0, 1, 4, 2, 3).reshape(L, 2, 64, 512)
    pack[:, :, 0:64, 64:576] = brT; pack[:, :, 64:128, 64:576] = biT
    pack[:, :, 0:64, 576:1088] = biT; pack[:, :, 64:128, 576:1088] = brT
    pack[:, :, 0:64, 1088:1600] = crT; pack[:, :, 64:128, 1088:1600] = ciT
    pack[:, :, 0:64, 1600:2112] = ciT; pack[:, :, 64:128, 1600:2112] = crT
    sh["s5pack"] = pack
    dd = f(inp["s5_d"]).reshape(L, 32, 16).transpose(0, 2, 1)
    sh["s5dd"] = f(np.tile(dd, (1, 8, 1)))
    sh["log_dt"] = f(inp["s5_log_dt"])
    glu = f(inp["s5_w_glu"])
    sh["glu_r"] = f(glu.reshape(L, 4, 128, 4, 128).transpose(0, 3, 2, 1, 4).reshape(L, 4, 128, 512))
    wa = f(inp["lru_w_a"]); wx = f(inp["lru_w_x"])
    gate = np.zeros((L, 4, 128, 2, 2, 128), np.float32)
    for q in range(4):
        for hh in range(2):
            h = 2 * q + hh
            gate[:, q, hh * 64:(hh + 1) * 64, :, 0, hh * 64:(hh + 1) * 64] = wa[:, :, h].transpose(0, 2, 1, 3)
            gate[:, q, hh * 64:(hh + 1) * 64, :, 1, hh * 64:(hh + 1) * 64] = wx[:, :, h].transpose(0, 2, 1, 3)
    sh["gate_r"] = f(gate.reshape(L, 4, 128, 512))
    sh["w_out_r"] = f(f(inp["w_out"]).reshape(L, 8, 128, D))
    wup = f(inp["ffn_w_up"])
    u = wup[:, :, :DFF].reshape(L, 8, 128, NFT, 128)
    v = wup[:, :, DFF:].reshape(L, 8, 128, NFT, 128)
    uv = np.concatenate([u, v], axis=-1)
    sh["w_up_r"] = f(uv.transpose(0, 3, 2, 1, 4).reshape(L, NFT, 128, 2048))
    sh["w_down_r"] = f(f(inp["ffn_w_down"]).reshape(L, NFT, 128, D))
    pvec = np.zeros((L, 128, NPV), np.float32)
    tp = lambda a, n: f(a).reshape(L, n, 128).transpose(0, 2, 1)
    pvec[:, :, 0:4] = tp(inp["s5_b_glu"], 4)
    cw = f(inp["lru_conv_w"]).reshape(L, 4, 4, 128)
    pvec[:, :, 4:20] = cw.transpose(0, 3, 2, 1).reshape(L, 128, 16)
    pvec[:, :, 20:24] = tp(inp["lru_conv_b"], 4)
    pvec[:, :, 24:32] = f(inp["lru_b_a"]).reshape(L, 2, 4, 128).transpose(0, 3, 1, 2).reshape(L, 128, 8)
    pvec[:, :, 32:40] = f(inp["lru_b_x"]).reshape(L, 2, 4, 128).transpose(0, 3, 1, 2).reshape(L, 128, 8)
    pvec[:, :, 40:48] = f(inp["lru_lam"]).reshape(L, 2, 4, 128).transpose(0, 3, 1, 2).reshape(L, 128, 8)
    fcw = f(inp["ffn_conv_w"]).reshape(L, 9, NFT, 128)
    pvec[:, :, 48:246] = fcw.transpose(0, 3, 2, 1).reshape(L, 128, NFT * 9)
    pvec[:, :, 246:268] = tp(inp["ffn_conv_b"], NFT)
    sh["pvec"] = pvec
    sh["lnrow"] = f(np.stack([f(inp["ln1_g"]), f(inp["ln1_b"]), f(inp["ln2_g"]), f(inp["ln2_b"])], axis=1))
    sh["consts"] = make_consts()
    return sh


def pack_core(inp, core):
    f = lambda a: np.ascontiguousarray(np.asarray(a, dtype=np.float32))
    b0 = 2 * core
    x = f(inp["x"][b0:b0 + 2]); ctx = f(inp["ctx"][b0:b0 + 2])
    xin = np.concatenate([ctx, x], axis=1)
    conds = np.stack([f(inp["c"][b0]), f(inp["c"][b0 + 1]), f(inp["c_ctx"])], axis=0)
    condT = conds.reshape(3, 8, 128).transpose(2, 1, 0).reshape(128, 24)
    return {"xin": f(xin), "condT": f(condT)}


_CACHE = {}


def kernel(**inputs):
    if "nc" not in _CACHE:
        _CACHE["nc"] = Builder().build()
    nc = _CACHE["nc"]
    sh = pack_shared(inputs)
    in_maps = []
    for c in range(NCORES):
        m = dict(sh)
        m.update(pack_core(inputs, c))
        in_maps.append(m)
    res = run_bass_kernel_spmd(nc, in_maps, core_ids=list(range(NCORES)))
    outs = [np.asarray(r["out"], dtype=np.float32) for r in res.results]
    return np.concatenate(outs, axis=0).reshape(16, 2048, D)
```

```python
import math
import os
import numpy as np
SKIP = os.environ.get('K_SKIP', '').split(',')
import concourse.bass as bass
import concourse.mybir as mybir
from concourse.bass_utils import run_bass_kernel_spmd

F32 = mybir.dt.float32
BF16 = mybir.dt.bfloat16
AF = mybir.ActivationFunctionType
ALU = mybir.AluOpType

NCORES = 8
DEPTH = 4
D = 1024
NT = 2304
NCTX = 256
TT = 18
NCH = 288
DFF = 2816
NFT = 22
ALPHA = 8.0 ** 0.25
EPS = 1e-6
NPV = 268
C_ID = 0; C_MF = 128; C_MB = 256; C_BAND = 384; C_EV = 384 + 8 * 240; C_SPM = C_EV + 16; C_SMP = C_SPM + 1
NCONST = C_SMP + 1
ESZ = {F32: 4, BF16: 2}


class Op:
    __slots__ = ("eng", "fn", "deps", "ddeps", "sig", "dma", "key", "semval", "idx")


class Prog:
    EPOCH = 6000

    def __init__(self, nc):
        self.nc = nc
        self.ops = []
        self.recs = {}
        self.dma_keys = {}

    @staticmethod
    def region(ap):
        t = ap.tensor
        name = t.name
        esz = ESZ[ap.dtype]
        pairs = ap.ap
        off = int(ap.offset)
        if str(ap.space) == "DRAM":
            lo = hi = off
            for st, cnt in pairs:
                if st >= 0:
                    hi += st * (cnt - 1)
                else:
                    lo += st * (cnt - 1)
            return (name, 0, 1, lo * esz, (hi + 1) * esz)
        row = 1
        for x in t.shape[1:]:
            row *= x
        p0 = off // row
        lo = hi = off - p0 * row
        pcnt = pairs[0][1]
        for st, cnt in pairs[1:]:
            if st >= 0:
                hi += st * (cnt - 1)
            else:
                lo += st * (cnt - 1)
        return (name, p0, p0 + pcnt, lo * esz, (hi + 1) * esz)

    def add(self, eng, fn, reads=(), writes=(), dma=False, key=None):
        op = Op()
        op.eng = eng; op.fn = fn; op.dma = dma; op.key = key; op.sig = False; op.semval = None
        op.idx = len(self.ops)
        deps = set()
        for ap in reads:
            name, p0, p1, b0, b1 = self.region(ap)
            lst = self.recs.setdefault(name, [])
            for r in lst:
                if r[5] and r[4] != op.idx and r[0] < p1 and p0 < r[1] and r[2] < b1 and b0 < r[3]:
                    deps.add(r[4])
            for r in lst:
                if r[4] == op.idx:
                    if (not r[5]) and r[0] == p0 and r[1] == p1 and r[2] == b0 and r[3] == b1:
                        break
                    continue
                if (not r[5]) and r[0] == p0 and r[1] == p1 and r[2] == b0 and r[3] == b1 \
                        and self.ops[r[4]].eng == eng and not self.ops[r[4]].dma and not dma:
                    r[4] = op.idx
                    break
            else:
                lst.append([p0, p1, b0, b1, op.idx, False])
        for ap in writes:
            name, p0, p1, b0, b1 = self.region(ap)
            lst = self.recs.setdefault(name, [])
            keep = []
            for r in lst:
                ov = r[0] < p1 and p0 < r[1] and r[2] < b1 and b0 < r[3]
                if ov and r[4] != op.idx:
                    deps.add(r[4])
                cov = p0 <= r[0] and r[1] <= p1 and b0 <= r[2] and r[3] <= b1
                if not (ov and cov):
                    keep.append(r)
            keep.append([p0, p1, b0, b1, op.idx, True])
            self.recs[name] = keep
        fd = []
        dd = {}
        for di in deps:
            d = self.ops[di]
            if d.dma:
                dd[d.key] = self.dma_keys[d.key] * 16
            elif dma or d.eng != eng:
                fd.append(di)
            elif eng != "pe":
                fd.append(di)
        op.deps = fd
        op.ddeps = dd
        if dma:
            c = self.dma_keys.get(key, 0) + 1
            self.dma_keys[key] = c
            op.semval = c * 16
        self.ops.append(op)
        return op

    def emit(self):
        nc = self.nc
        ops = self.ops
        for op in ops:
            best = {}
            for di in op.deps:
                d = ops[di]
                if d.eng not in best or best[d.eng] < di:
                    best[d.eng] = di
            op.deps = list(best.values())
            for di in op.deps:
                ops[di].sig = True
        cnt = {}
        sems = {}
        for op in ops:
            if op.dma or not op.sig:
                continue
            c = cnt.get(op.eng, 0)
            ep = c // self.EPOCH
            k = (op.eng, ep)
            if k not in sems:
                sems[k] = nc.alloc_semaphore(f"s_{op.eng}_{ep}")
            op.semval = (sems[k], c - ep * self.EPOCH + 1)
            cnt[op.eng] = c + 1
        dsems = {}
        for key in self.dma_keys:
            dsems[key] = nc.alloc_semaphore(f"d_{key}")
        for op in ops:
            if op.dma:
                op.semval = (dsems[op.key], op.semval)
        self.nsem = len(sems) + len(dsems)
        engs = {"pe": [], "act": [], "dve": [], "pool": [], "sp": []}
        for op in ops:
            engs[op.eng].append(op)

        def run(eobj, lst):
            known = {}
            for op in lst:
                for di in op.deps:
                    sem, val = ops[di].semval
                    if known.get(sem.name, 0) < val:
                        eobj.wait_ge(sem, val)
                        known[sem.name] = val
                for key, val in op.ddeps.items():
                    sem = dsems[key]
                    if known.get(sem.name, 0) < val:
                        eobj.wait_ge(sem, val)
                        known[sem.name] = val
                ins = op.fn(eobj)
                if op.dma:
                    ins.then_inc(op.semval[0], 16)
                elif op.sig:
                    ins.then_inc(op.semval[0], 1)
            return known

        with nc.Block() as block:
            @block.tensor
            def _(e):
                run(e, engs["pe"])

            @block.scalar
            def _(e):
                run(e, engs["act"])

            @block.vector
            def _(e):
                run(e, engs["dve"])

            @block.gpsimd
            def _(e):
                run(e, engs["pool"])

            @block.sync
            def _(e):
                known = run(e, engs["sp"])
                for key, c in self.dma_keys.items():
                    sem = dsems[key]
                    if known.get(sem.name, 0) < c * 16:
                        e.wait_ge(sem, c * 16)


class Builder:
    def __init__(self, nlayers=DEPTH, nseq=2, taps=None, limit=99):
        self.limit = limit
        self.nlayers = nlayers
        self.nseq = nseq
        self.taps = taps or []
        nc = bass.Bass("TRN2", target_bir_lowering=False)
        self.nc = nc
        self.P = Prog(nc)
        self.psn = 0
        self.kpn = 0
        self.dram = {}
        self._declare()
        self._alloc()

    def din(self, name, shape, dt=F32):
        self.dram[name] = self.nc.dram_tensor(name, list(shape), dt, kind="ExternalInput").ap()
        return self.dram[name]

    def dscr(self, name, shape, dt=F32, kind="Internal"):
        self.dram[name] = self.nc.dram_tensor(name, list(shape), dt, kind=kind).ap()
        return self.dram[name]

    def _declare(self):
        L = self.nlayers
        self.din("xin", [2, NT, D])
        self.din("condT", [128, 24])
        self.din("w_mod", [L, D, 6 * D])
        self.din("b_mod", [L, 6 * D])
        self.din("w_in_r", [L, 12, 128, 8 * 128])
        self.din("s5pack", [L, 2, 128, 2112])
        self.din("s5dd", [L, 128, 32])
        self.din("log_dt", [L, 2, 32])
        self.din("glu_r", [L, 4, 128, 4 * 128])
        self.din("gate_r", [L, 4, 128, 4 * 128])
        self.din("w_out_r", [L, 8, 128, D])
        self.din("w_up_r", [L, NFT, 128, 8 * 256])
        self.din("w_down_r", [L, NFT, 128, D])
        self.din("pvec", [L, 128, NPV])
        self.din("lnrow", [L, 4, D])
        self.din("consts", [128, NCONST])
        self.dscr("out", [2, 2048, D], kind="ExternalOutput")
        self.dscr("xs_d", [2, NT, D])
        self.dscr("ada_d", [L, 3, 6 * D])
        self.dscr("s5G_d", [L, 32, 128, 128], BF16)
        self.dscr("s5Win_d", [L, 2, 32, 128, 128], BF16)
        self.dscr("s5Mout_d", [L, 2, 2, 64, 32, 128])
        self.dscr("s5mu_d", [L, 2, 2, 64, 32])
        for name, shape, dt in self.taps:
            self.dscr("tap_" + name, shape, dt, kind="ExternalOutput")

    def _alloc(self):
        nc = self.nc
        self.cst = nc.alloc_sbuf_tensor("cst", [128, NCONST], F32)
        self.identb = nc.alloc_sbuf_tensor("identb", [128, 128], BF16)
        self.bandb = nc.alloc_sbuf_tensor("bandb", [128, 8, 240], BF16)
        self.adaT = nc.alloc_sbuf_tensor("adaT", [128, DEPTH, 48, 3], F32)
        self.pv = nc.alloc_sbuf_tensor("pv", [128, 2, NPV], F32)
        self.pv2 = nc.alloc_sbuf_tensor("pv2", [128, 2, 32], F32)
        self.m12 = nc.alloc_sbuf_tensor("m12", [128, 2, 64], F32)
        self.small = nc.alloc_sbuf_tensor("small", [128, 64], F32)
        self.sc3 = nc.alloc_sbuf_tensor("sc3", [128, 3, 64], F32)
        rem = nc.sbuf_bytes_remaining
        self.ASZ = (rem - 2048) // 64 * 64
        self.arena = nc.alloc_sbuf_tensor("arena", [128, self.ASZ // 4], F32)
        self.ps = nc.alloc_psum_tensor("ps", [128, 4096], F32)

    def av(self, off, n, dt=F32):
        assert off % 4 == 0 and off + n * ESZ[dt] <= self.ASZ, (off, n, dt, self.ASZ)
        a = self.arena[:, off // 4:(off + n * ESZ[dt] + 3) // 4]
        if dt == BF16:
            a = a.bitcast(BF16)
            a = a[:, 0:n]
        return a

    def bank(self, n=1):
        if n == 2 and self.psn % 2 == 1:
            self.psn += 1
        b = self.psn % 8
        self.psn += n
        return self.ps[:, b * 512:(b + n) * 512]

    def mm(self, out, lhsT, rhs, start=True, stop=True):
        self.P.add("pe", lambda e: e.matmul(out, lhsT, rhs, start=start, stop=stop),
                   reads=[lhsT, rhs], writes=[out])

    def tr(self, out, in_, ident):
        self.P.add("pe", lambda e: e.transpose(out, in_, ident), reads=[in_, ident], writes=[out])

    def act(self, out, in_, func, bias=0.0, scale=1.0, eng="act"):
        rd = [in_]
        if not isinstance(bias, (int, float)):
            rd.append(bias)
        if not isinstance(scale, (int, float)):
            rd.append(scale)
        self.P.add("act", lambda e: e.activation(out=out, in_=in_, func=func, bias=bias, scale=scale),
                   reads=rd, writes=[out])

    def tt(self, out, in0, in1, op, eng="dve"):
        self.P.add(eng, lambda e: e.tensor_tensor(out, in0, in1, op), reads=[in0, in1], writes=[out])

    def ts(self, out, in0, s1, s2, op0, op1=None, eng="dve"):
        rd = [in0]
        if not isinstance(s1, (int, float)):
            rd.append(s1)
        if s2 is not None and not isinstance(s2, (int, float)):
            rd.append(s2)
        if op1 is None:
            self.P.add(eng, lambda e: e.tensor_scalar(out, in0, s1, None, op0), reads=rd, writes=[out])
        else:
            self.P.add(eng, lambda e: e.tensor_scalar(out, in0, s1, s2, op0, op1), reads=rd, writes=[out])

    def stt(self, out, in0, scalar, in1, op0, op1, eng="dve"):
        rd = [in0, in1]
        if not isinstance(scalar, (int, float)):
            rd.append(scalar)
        self.P.add(eng, lambda e: e.scalar_tensor_tensor(out, in0, scalar, in1, op0, op1), reads=rd, writes=[out])

    def cp(self, out, in_, eng="dve"):
        if eng == "act":
            self.P.add("act", lambda e: e.copy(out, in_), reads=[in_], writes=[out])
        else:
            self.P.add(eng, lambda e: e.tensor_copy(out, in_), reads=[in_], writes=[out])

    def memset(self, out, val, eng="dve"):
        self.P.add(eng, lambda e: e.memset(out, val), reads=[], writes=[out])

    def dma(self, out, in_, key, q="sp", slow=False):
        if slow:
            fn = lambda e: e.dma_start(out=out, in_=in_, allow_slow_non_contiguous=True)
        else:
            fn = lambda e: e.dma_start(out=out, in_=in_)
        self.P.add(q, fn, reads=[in_], writes=[out], dma=True, key=key)

    def tap(self, name, sb_ap):
        if ("tap_" + name) in self.dram:
            self.dma(self.dram["tap_" + name], sb_ap, key="tap")

    def build(self):
        self.load_consts()
        if self.limit >= 1:
            self.prologue_ada()
        if self.limit >= 2:
            for l in range(self.nlayers):
                self.prologue_s5(l)
        if self.limit >= 3:
            for s in range(self.nseq):
                for l in range(self.nlayers):
                    self.layer(s, l)
        if "tap_xs" in self.dram:
            for s_ in range(2):
                for t_ in range(TT):
                    self.dma(self.dram["tap_xs"][s_, t_ * 128:(t_ + 1) * 128, :], self.dram["xs_d"][s_, t_ * 128:(t_ + 1) * 128, :], key="tap")
        self.P.emit()
        return self.nc

    def load_consts(self):
        d = self.dram
        self.dma(self.cst[:], d["consts"], key="cst")
        self.cp(self.identb[:], self.cst[:, C_ID:C_ID + 128])
        self.cp(self.bandb[:].rearrange("p a b -> p (a b)"), self.cst[:, C_BAND:C_BAND + 1920])

    def prologue_ada(self):
        d = self.dram
        A = 0
        condT = self.av(A, 24); A += 96
        siluT = self.av(A, 24, BF16); A += 64
        bm3 = self.av(A, 6 * D); A += 6 * D * 4
        adarow = self.av(A, 6 * D); A += 6 * D * 4
        wm = [self.av(A + i * 8192, 4096, BF16) for i in range(2)]; A += 16384
        self.dma(condT, d["condT"], key="condT")
        self.act(siluT, condT, AF.Silu)
        siluT3 = siluT.rearrange("p (k c) -> p k c", c=3)
        ident = self.cst[:, C_ID:C_ID + 128]
        for l in range(self.nlayers):
            self.dma(bm3[0:3, :], d["b_mod"][l:l + 1, :].partition_broadcast(3).rearrange("p a n -> p (a n)"), key="bm3")
            for cc in range(12):
                w = wm[cc % 2]
                w3 = w.rearrange("p (k n) -> p k n", k=8)
                self.dma(w3, d["w_mod"][l][:, cc * 512:(cc + 1) * 512].rearrange("(k p) n -> p k n", p=128),
                         key=f"wm{cc % 2}", q="pool")
                ps = self.bank()
                for kt in range(8):
                    self.mm(ps[0:3, :], siluT3[:, kt, :], w3[:, kt, :], start=(kt == 0), stop=(kt == 7))
                self.tt(adarow[0:3, cc * 512:(cc + 1) * 512], ps[0:3, :], bm3[0:3, cc * 512:(cc + 1) * 512], ALU.add)
            self.dma(d["ada_d"][l], adarow[0:3, :], key="ada_st")
            ps = self.bank()
            for j in range(48):
                self.tr(ps[:, j * 3:(j + 1) * 3], adarow[0:3, j * 128:(j + 1) * 128], ident[0:3, 0:3])
            aT = self.adaT[:, l].rearrange("p a c -> p (a c)")
            self.cp(aT, ps[:, 0:144])
            for mod in (1, 4):
                v = self.adaT[:, l, mod * 8:(mod + 1) * 8, :].rearrange("p a c -> p (a c)")
                self.ts(v, v, 1.0, None, ALU.add)

    def prologue_s5(self, l):
        d = self.dram
        cst = self.cst
        sgn_pm = cst[:, C_SPM:C_SPM + 1]
        sgn_mp = cst[:, C_SMP:C_SMP + 1]
        EV = cst[:, C_EV:C_EV + 16]
        identf = cst[:, C_ID:C_ID + 128]
        A = [0]

        def al(n, dt=F32):
            v = self.av(A[0], n, dt)
            A[0] += (n * ESZ[dt] + 63) // 64 * 64
            return v

        Q = [al(4096), al(4096)]
        CF = [al(4096), al(4096)]
        SP = al(2112)
        DTb = al(32); dtt = al(32); Aa = al(32); PHI = al(32)
        MAG = al(512); ANG = al(512); TMP = al(512); SN = al(512); CS = al(512); LRe = al(512); LIe = al(512)
        t32 = [al(32) for _ in range(8)]
        P1 = al(512); P2 = al(512); P1n = al(512); P2n = al(512); C2s = al(512); C1pm = al(512); C2n = al(512)
        T1 = al(4096); T2 = al(4096); W = al(4096)
        Wb = al(4096, BF16)
        Gs = al(512); Gs2 = al(512)
        Gb = al(4096, BF16)
        Dd = al(32)
        v3 = lambda x: x.rearrange("p (g c) -> p g c", g=32)
        v4 = lambda x: x.rearrange("p (g s c) -> p g s c", g=32, s=8)
        TWO_PI = 2.0 * math.pi
        for dd in range(2):
            self.dma(SP, d["s5pack"][l, dd], key="s5pack")
            self.dma(DTb, d["log_dt"][l, dd:dd + 1, :].partition_broadcast(128).rearrange("p a n -> p (a n)"), key="s5dt")
            LR = SP[:, 0:32]; LI = SP[:, 32:64]
            B1 = SP[:, 64:576]; B2 = SP[:, 576:1088]; C1 = SP[:, 1088:1600]; C2 = SP[:, 1600:2112]
            self.act(dtt, DTb, AF.Exp)
            self.tt(Aa, LR, dtt, ALU.mult)
            self.tt(PHI, LI, dtt, ALU.mult)
            bc_ge = lambda x: x.unsqueeze(2).to_broadcast([128, 32, 16])
            ev_b = EV.unsqueeze(1).to_broadcast([128, 32, 16])
            self.tt(v3(MAG), bc_ge(Aa), ev_b, ALU.mult)
            self.act(MAG, MAG, AF.Exp)
            self.tt(v3(ANG), bc_ge(PHI), ev_b, ALU.mult)
            MAGIC = 12582912.0
            self.ts(TMP, ANG, 1.0 / TWO_PI, MAGIC, ALU.mult, ALU.add)
            self.ts(TMP, TMP, -MAGIC, None, ALU.add)
            self.stt(TMP, TMP, -TWO_PI, ANG, ALU.mult, ALU.add)
            self.act(SN, TMP, AF.Sin)
            self.ts(ANG, ANG, 0.5 * math.pi, None, ALU.add)
            self.ts(TMP, ANG, 1.0 / TWO_PI, MAGIC, ALU.mult, ALU.add)
            self.ts(TMP, TMP, -MAGIC, None, ALU.add)
            self.stt(TMP, TMP, -TWO_PI, ANG, ALU.mult, ALU.add)
            self.act(CS, TMP, AF.Sin)
            self.tt(LRe, MAG, CS, ALU.mult)
            self.tt(LIe, MAG, SN, ALU.mult)
            LRe3 = v3(LRe); LIe3 = v3(LIe)
            nr, den, rden, kr, ki, u1, u2, krs = t32
            self.ts(nr, LRe3[:, :, 8], -1.0, None, ALU.add)
            l1i = LIe3[:, :, 8]
            self.tt(den, LR, LR, ALU.mult)
            self.tt(u1, LI, LI, ALU.mult)
            self.tt(den, den, u1, ALU.add)
            self.P.add("dve", lambda e, o=rden, i=den: e.reciprocal(o, i), reads=[den], writes=[rden])
            self.tt(u1, nr, LR, ALU.mult)
            self.tt(u2, l1i, LI, ALU.mult)
            self.tt(u1, u1, u2, ALU.add)
            self.tt(kr, u1, rden, ALU.mult)
            self.tt(u1, l1i, LR, ALU.mult)
            self.tt(u2, nr, LI, ALU.mult)
            self.tt(u1, u1, u2, ALU.subtract)
            self.tt(ki, u1, rden, ALU.mult)
            kis = u1
            self.ts(kis, ki, sgn_mp, None, ALU.mult)
            self.ts(krs, kr, sgn_mp, None, ALU.mult)
            self.tt(v3(P1), bc_ge(kr), v3(B1), ALU.mult)
            self.tt(v3(TMP), bc_ge(kis), v3(B2), ALU.mult)
            self.tt(P1, P1, TMP, ALU.add)
            self.tt(v3(P2), bc_ge(krs), v3(B2), ALU.mult)
            self.tt(v3(TMP), bc_ge(ki), v3(B1), ALU.mult)
            self.tt(P2, P2, TMP, ALU.subtract)
            self.ts(P1n, P1, sgn_pm, None, ALU.mult)
            self.ts(P2n, P2, sgn_pm, None, ALU.mult)
            self.ts(C2s, C2, sgn_mp, None, ALU.mult)
            self.ts(C1pm, C1, sgn_pm, None, ALU.mult)
            self.ts(C2n, C2, -1.0, None, ALU.mult)

            def esl(tab3, e0, step):
                if step > 0:
                    return tab3[:, :, e0:e0 + 8]
                stop = e0 - 8
                return tab3[:, :, e0:(stop if stop >= 0 else None):-1]

            def build(dst, e0, step, PA, PB):
                lr = esl(LRe3, e0, step).unsqueeze(3).to_broadcast([128, 32, 8, 16])
                li = esl(LIe3, e0, step).unsqueeze(3).to_broadcast([128, 32, 8, 16])
                pa = v3(PA).unsqueeze(2).to_broadcast([128, 32, 8, 16])
                pb = v3(PB).unsqueeze(2).to_broadcast([128, 32, 8, 16])
                self.tt(v4(T1), lr, pa, ALU.mult)
                self.tt(v4(T2), li, pb, ALU.mult)
                self.tt(dst, T1, T2, ALU.add)

            if dd == 0:
                build(W, 14, -1, P1, P2)
            else:
                build(W, 7, +1, P1, P2)
            W3 = W.rearrange("p (g m) -> p g m", g=32)
            Wb3 = Wb.rearrange("p (g m) -> p g m", g=32)
            for g4 in range(8):
                ps = self.bank()
                for k in range(4):
                    self.tr(ps[:, k * 128:(k + 1) * 128], W3[:, g4 * 4 + k, :], identf)
                self.cp(Wb[:, g4 * 512:(g4 + 1) * 512], ps, eng=("act" if g4 % 2 else "dve"))
            self.dma(d["s5Win_d"][l, dd].rearrange("g p m -> p g m"), Wb3, key="s5st")
            if dd == 0:
                build(Q[dd], 7, -1, P1n, P2n)
                build(CF[dd], 7, +1, C1, C2s)
            else:
                build(Q[dd], 7, +1, P1n, P2n)
                build(CF[dd], 7, -1, C1, C2s)
            if dd == 0:
                build(W, 8, +1, C1pm, C2n)
            else:
                build(W, 15, -1, C1pm, C2n)
            for ri in range(2):
                self.dma(d["s5Mout_d"][l, dd, ri].rearrange("p g m -> p g m"), W3[ri * 64:(ri + 1) * 64, :, :], key="s5st")
            self.dma(d["s5mu_d"][l, dd, 0], LRe3[0:64, :, 15], key="s5st", slow=True)
            self.dma(d["s5mu_d"][l, dd, 1], LIe3[0:64, :, 15], key="s5st", slow=True)
        self.dma(Dd, d["s5dd"][l], key="s5dd")
        maskF = self.cst[:, C_MF:C_MF + 128].unsqueeze(1).to_broadcast([128, 4, 128])
        maskB = self.cst[:, C_MB:C_MB + 128].unsqueeze(1).to_broadcast([128, 4, 128])
        Q3 = [q.rearrange("p (g m) -> p g m", g=32) for q in Q]
        CF3 = [c.rearrange("p (g m) -> p g m", g=32) for c in CF]
        Gb3 = Gb.rearrange("p (g m) -> p g m", g=32)
        for g4 in range(8):
            psF = self.bank(); psB = self.bank()
            for k in range(4):
                g = g4 * 4 + k
                self.mm(psF[:, k * 128:(k + 1) * 128], Q3[0][:, g, :], CF3[0][:, g, :])
                self.mm(psB[:, k * 128:(k + 1) * 128], Q3[1][:, g, :], CF3[1][:, g, :])
            f4 = lambda x: x.rearrange("p (a m) -> p a m", a=4)
            self.tt(f4(Gs), f4(psF), maskF, ALU.mult)
            self.tt(f4(Gs2), f4(psB), maskB, ALU.mult)
            self.tt(Gs, Gs, Gs2, ALU.add)
            for k in range(4):
                g = g4 * 4 + k
                self.stt(Gb3[:, g, :], identf, Dd[:, g:g + 1], Gs[:, k * 128:(k + 1) * 128], ALU.mult, ALU.add)
        self.dma(d["s5G_d"][l].rearrange("g p m -> p g m"), Gb3, key="s5st")

    def ln_stats(self, x, slot, eps=EPS):
        base = slot * 16
        st = self.small[:, base:base + 12].rearrange("p (a b) -> p a b", a=2)
        mv = self.small[:, base + 12:base + 14]
        rstd = self.small[:, base + 14:base + 15]
        nmr = self.small[:, base + 15:base + 16]
        for h in range(2):
            self.P.add("dve", lambda e, o=st[:, h, :], i=x[:, h * 512:(h + 1) * 512]: e.bn_stats(o, i),
                       reads=[x[:, h * 512:(h + 1) * 512]], writes=[st[:, h, :]])
        self.P.add("dve", lambda e, o=mv, i=st: e.bn_aggr(o, i), reads=[st], writes=[mv])
        self.act(rstd, mv[:, 1:2], AF.Sqrt, bias=eps)
        self.P.add("dve", lambda e: e.reciprocal(rstd, rstd), reads=[rstd], writes=[rstd])
        self.stt(nmr, mv[:, 0:1], -1.0, rstd, ALU.mult, ALU.mult)
        return rstd, nmr

    def ln_mod_T(self, s, l, tiles, hT, col0, mod_shift, mod_scale, A0, src_name="xs_d"):
        d = self.dram
        xt = [self.av(A0 + i * 4096, 1024) for i in range(2)]
        xn = [self.av(A0 + 8192 + i * 2048, 1024, BF16) for i in range(4)]
        groups = []
        cur = []
        for t in tiles:
            cond = 2 if t < 2 else s
            if cur and (len(cur) == 4 or cur[0][1] != cond):
                groups.append(cur); cur = []
            cur.append((t, cond))
        if cur:
            groups.append(cur)
        pos = 0
        n = 0
        for grp in groups:
            cond = grp[0][1]
            ng = len(grp)
            for gi, (t, _) in enumerate(grp):
                x = xt[n % 2]; xb = xn[gi]
                self.dma(x, d[src_name][s, t * 128:(t + 1) * 128, :], key=f"lnx{n % 2}")
                rstd, nmr = self.ln_stats(x, n % 4)
                self.act(xb, x, AF.Identity, bias=nmr, scale=rstd)
                n += 1
            for kt in range(8):
                pb = self.bank()
                for gi in range(ng):
                    self.mm(pb[:, gi * 128:(gi + 1) * 128], xn[gi][:, kt * 128:(kt + 1) * 128], self.identb[:])
                src = pb[:, 0:ng * 128]
                dst = hT[:, kt, col0 + pos:col0 + pos + ng * 128]
                sc = self.adaT[:, l, mod_scale * 8 + kt, cond:cond + 1]
                sh = self.adaT[:, l, mod_shift * 8 + kt, cond:cond + 1]
                if kt % 2 == 0:
                    self.act(dst, src, AF.Identity, bias=sh, scale=sc)
                else:
                    self.ts(dst, src, sc, sh, ALU.mult, ALU.add)
            pos += ng * 128

    def layer(self, s, l):
        d = self.dram
        last = (l == DEPTH - 1)
        slot = l % 2
        pv = self.pv[:, slot, :]
        self.dma(pv, d["pvec"][l], key=f"pv{slot}")
        PV_GLUB = 0; PV_LCW = 4; PV_LCB = 20; PV_LBA = 24; PV_LBX = 32; PV_LLAM = 40; PV_FCW = 48; PV_FCB = 246
        coef = self.pv2[:, slot, 0:8]
        hcf = self.pv2[:, slot, 8:16]
        hba = self.pv2[:, slot, 16:24]
        hbx = self.pv2[:, slot, 24:32]
        self.act(coef, pv[:, PV_LLAM:PV_LLAM + 8], AF.Exp, scale=-1.0)
        self.act(coef, coef, AF.Ln, bias=1.0)
        self.ts(coef, coef, -8.0, None, ALU.mult)
        self.ts(hcf, coef, 0.5, None, ALU.mult)
        self.ts(hba, pv[:, PV_LBA:PV_LBA + 8], 0.5, None, ALU.mult)
        self.ts(hbx, pv[:, PV_LBX:PV_LBX + 8], 0.5, None, ALU.mult)

        H1T = 0
        YALL = 36864
        UALL = 73728
        TMPB = 92160
        h1T = self.av(H1T, 8 * NT, BF16).rearrange("p (k t) -> p k t", k=8)
        yall = self.av(YALL, 8 * NT, BF16).rearrange("p (k t) -> p k t", k=8)
        uall = self.av(UALL, 32 * NCH, BF16).rearrange("p (g m) -> p g m", g=32)

        self.ln_mod_T(s, l, list(range(TT)), h1T, 0, mod_shift=0, mod_scale=1, A0=TMPB,
                      src_name=("xin" if l == 0 else "xs_d"))
        if l == 0 and s == 0:
            self.tap("h1T", h1T.rearrange("p k t -> p (k t)"))

        if self.limit < 4:
            return
        chunks = [(0, 512), (512, 512), (1024, 512), (1536, 512), (2048, 256)]

        def load_win(ft, slot_i):
            w = self.av(TMPB + 16384 + slot_i * 2048, 1024, BF16)
            self.dma(w, d["w_in_r"][l, ft], key=f"win{slot_i}", q="pool")
            return w.rearrange("p (k c) -> p k c", k=8)

        ufm = [self.av(TMPB + i * 4608, NT, BF16) for i in range(2)]

        def s5proj(q):
            w_u = load_win(q, 2)
            u = ufm[q % 2]
            for ci, (c0, cn) in enumerate(chunks):
                ps = self.bank()
                for kt in range(8):
                    self.mm(ps[:, 0:cn], w_u[:, kt, :], h1T[:, kt, c0:c0 + cn], start=(kt == 0), stop=(kt == 7))
                self.cp(u[:, c0:c0 + cn], ps[:, 0:cn], eng="dve")
            for g8 in range(8):
                ps = self.bank()
                for j in range(8):
                    self.mm(ps[:, 0:NCH], self.bandb[:, g8, 112 - 16 * j:240 - 16 * j], u[:, j:NT:8],
                            start=(j == 0), stop=(j == 7))
                self.cp(uall[:, q * 8 + g8, :], ps[:, 0:NCH], eng="dve")

        B = TMPB + 16384 + 6144
        xlp = self.av(B, 2310); B += 9280
        xc = self.av(B, NT); B += 9216
        xcb = self.av(B, NT, BF16); B += 4608
        hsum = self.av(B, NT); B += 9216
        gw = self.av(B, 512, BF16); B += 1024
        afull = self.av(B, NT); B += 9216
        bfull = self.av(B, NT); B += 9216
        sfull = self.av(B, NT); B += 9216
        ctmp = [self.av(B + i * 2048, 512) for i in range(4)]; B += 4 * 2048
        gg = [self.av(B + i * 1024, 512, BF16) for i in range(2)]; B += 2048
        assert B <= self.ASZ, (B, self.ASZ)
        XC0 = 2; XL0 = 261
        for q in range(4):
            w_x = load_win(4 + q, 0)
            w_g = load_win(8 + q, 1)
            gw4 = gw.rearrange("p (a c) -> p a c", a=4)
            self.dma(gw, d["gate_r"][l, q], key="gatew", q="pool")
            self.memset(xlp[:, 0:2], 0.0)
            self.memset(xlp[:, 258:261], 0.0)
            self.memset(xlp[:, 2309:2310], 0.0)
            for ci, (c0, cn) in enumerate(chunks):
                ps = self.bank()
                for kt in range(8):
                    self.mm(ps[:, 0:cn], w_x[:, kt, :], h1T[:, kt, c0:c0 + cn], start=(kt == 0), stop=(kt == 7))
                if c0 == 0:
                    self.cp(xlp[:, XC0:XC0 + 256], ps[:, 0:256], eng="act")
                    self.cp(xlp[:, XL0:XL0 + 256], ps[:, 256:512], eng="act")
                else:
                    self.cp(xlp[:, XL0 + c0 - 256:XL0 + c0 - 256 + cn], ps[:, 0:cn], eng="act")
            s5proj(q)
            for (o0, on, i0) in ((0, 256, XC0 - 2), (256, 2048, XL0 - 2)):
                cw = lambda k: pv[:, PV_LCW + q * 4 + k:PV_LCW + q * 4 + k + 1]
                self.ts(xc[:, o0:o0 + on], xlp[:, i0:i0 + on], cw(0), pv[:, PV_LCB + q:PV_LCB + q + 1], ALU.mult, ALU.add)
                for k in range(1, 4):
                    self.stt(xc[:, o0:o0 + on], xlp[:, i0 + k:i0 + k + on], cw(k), xc[:, o0:o0 + on], ALU.mult, ALU.add)
            self.cp(xcb, xc, eng="act")
            if l == 0 and s == 0 and q == 0:
                self.tap("xc0", xc)
            for dd in range(2):
                c_hcf = hcf[:, dd * 4 + q:dd * 4 + q + 1]
                c_hba = hba[:, dd * 4 + q:dd * 4 + q + 1]
                c_hbx = hbx[:, dd * 4 + q:dd * 4 + q + 1]
                for ci, (c0, cn) in enumerate(chunks):
                    psr = self.bank(); psi = self.bank()
                    self.mm(psr[:, 0:cn], gw4[:, dd * 2 + 0, :], xcb[:, c0:c0 + cn])
                    self.mm(psi[:, 0:cn], gw4[:, dd * 2 + 1, :], xcb[:, c0:c0 + cn])
                    t1 = ctmp[(ci % 2) * 2]; t2 = ctmp[(ci % 2) * 2 + 1]
                    self.act(t1[:, 0:cn], psr[:, 0:cn], AF.Tanh, bias=c_hba, scale=0.5)
                    self.act(afull[:, c0:c0 + cn], t1[:, 0:cn], AF.Exp, bias=c_hcf, scale=c_hcf)
                    self.act(t2[:, 0:cn], psi[:, 0:cn], AF.Tanh, bias=c_hbx, scale=0.5)
                    self.stt(bfull[:, c0:c0 + cn], t2[:, 0:cn], 1.0, xc[:, c0:c0 + cn], ALU.add, ALU.mult)
                self.act(sfull, afull, AF.Square)
                self.act(sfull, sfull, AF.Sqrt, bias=1.0, scale=-1.0)
                self.stt(bfull, sfull, 0.5, bfull, ALU.mult, ALU.mult)

                def scan(o, a, b, init, rev):
                    rd = [a, b] + ([] if isinstance(init, float) else [init])
                    if rev:
                        self.P.add("dve", lambda e: e.tensor_tensor_scan(o[:, ::-1], a[:, ::-1], b[:, ::-1], init,
                                                                         ALU.mult, ALU.add), reads=rd, writes=[o])
                    else:
                        self.P.add("dve", lambda e: e.tensor_tensor_scan(o, a, b, init, ALU.mult, ALU.add),
                                   reads=rd, writes=[o])

                if dd == 0:
                    scan(hsum, afull, bfull, 0.0, False)
                else:
                    scan(sfull[:, 0:256], afull[:, 0:256], bfull[:, 0:256], 0.0, True)
                    scan(sfull[:, 256:NT], afull[:, 256:NT], bfull[:, 256:NT], sfull[:, 0:1], True)
                    self.tt(hsum, hsum, sfull, ALU.add)
            for ci, (c0, cn) in enumerate(chunks):
                ps = self.bank()
                for kt in range(8):
                    self.mm(ps[:, 0:cn], w_g[:, kt, :], h1T[:, kt, c0:c0 + cn], start=(kt == 0), stop=(kt == 7))
                g = gg[ci % 2]
                self.act(g[:, 0:cn], ps[:, 0:cn], AF.Gelu_apprx_tanh)
                self.tt(yall[:, 4 + q, c0:c0 + cn], hsum[:, c0:c0 + cn], g[:, 0:cn], ALU.mult)
            if l == 0 and s == 0 and q == 0:
                self.tap("hsum0", hsum)

        if self.limit < 5:
            return
        if self.limit < 6:
            return
        YG = 0
        RING = 18432
        ZS = TMPB
        yg = self.av(YG, 4 * NT, BF16).rearrange("p (k t) -> p k t", k=4)
        zs = self.av(ZS, NCH * 64).rearrange("p (i c) -> p i c", c=64)
        ZE = ZS + NCH * 64 * 4
        ys = [self.av(ZE + i * 4608, 8 * NCH, BF16).rearrange("p (g m) -> p g m", g=8) for i in range(1)]
        assert ZE + 4608 <= self.ASZ
        m1 = self.m12[:, 0, :]; m2 = self.m12[:, 1, :]
        if s == 0 or True:
            mu = d["s5mu_d"][l]
            for gp in range(2):
                for dd in range(2):
                    for ri in range(2):
                        c0 = ri * 32 + dd * 16
                        src_re = mu[dd, 0][:, gp:32:2]
                        src_im = mu[dd, 1][:, gp:32:2]
                        self.dma(m1[gp * 64:(gp + 1) * 64, c0:c0 + 16], src_re, key="mu", slow=True)
                        self.dma(m2[gp * 64:(gp + 1) * 64, c0:c0 + 16], src_im, key="mu", slow=True)
            self.ts(m2[:, 32:64], m2[:, 32:64], -1.0, None, ALU.mult)
        RSZ = 4608
        for g2 in range(16):
            rs = RING + (g2 % 3) * RSZ
            gt = self.av(rs, 256, BF16).rearrange("p (a m) -> p a m", a=2)
            wt = self.av(rs + 512, 512, BF16).rearrange("p (a b m) -> p a b m", a=2, b=2)
            self.dma(wt[:, :, 0, :], d["s5Win_d"][l, 0, 2 * g2:2 * g2 + 2].rearrange("g p m -> p g m"), key=f"s5w{g2 % 3}")
            self.dma(wt[:, :, 1, :], d["s5Win_d"][l, 1, 2 * g2:2 * g2 + 2].rearrange("g p m -> p g m"), key=f"s5w{g2 % 3}")
            for dd in range(2):
                for ri in range(2):
                    ps = self.bank()
                    for gp in range(2):
                        g = 2 * g2 + gp
                        o = ps[gp * 64:(gp + 1) * 64, :]
                        lhs = wt[:, gp, dd, ri * 64:(ri + 1) * 64]
                        if dd == 0:
                            self.mm(o[:, 0:NCH], lhs, uall[:, g, :])
                        else:
                            self.mm(o[:, 0:32], lhs, uall[:, g, 31::-1])
                            self.mm(o[:, 32:NCH], lhs, uall[:, g, NCH - 1:31:-1])
                    col = ri * 32 + dd * 16 + g2
                    self.cp(zs[:, :, col], ps[:, 0:NCH], eng=("act" if (dd + ri) % 2 else "dve"))
        mcat = self.m12[:]
        m2c = self.av(ZE + 4608, 128).rearrange("p (a c) -> p a c", a=2)
        tq = self.av(ZE + 4608 + 512, 128)
        self.tt(tq[:, 0:64], mcat[:, 0, :], mcat[:, 0, :], ALU.mult)
        self.tt(tq[:, 64:128], mcat[:, 1, :], mcat[:, 1, :], ALU.mult)
        self.tt(m2c[:, 0, :], tq[:, 0:64], tq[:, 64:128], ALU.subtract)
        self.tt(tq[:, 0:64], mcat[:, 0, :], mcat[:, 1, :], ALU.mult)
        self.ts(m2c[:, 1, :], tq[:, 0:64], 2.0, None, ALU.mult)
        PB = ZE + 4608 + 1024
        NBK = 24
        pblk = self.av(PB, NBK * 128).rearrange("p (i a c) -> p i a c", a=2, c=64)
        assert PB + NBK * 128 * 4 <= self.ASZ, (PB, self.ASZ)
        hi = NCH
        while hi > 1:
            lo = max(1, hi - NBK)
            n = hi - lo
            zprev = zs[:, lo - 1:hi - 1, :]
            self.tt(pblk[:, 0:n], mcat.unsqueeze(1).to_broadcast([128, n, 2, 64]),
                    zprev.unsqueeze(2).to_broadcast([128, n, 2, 64]), ALU.mult)
            plo = pblk[:, 0:n, 0, :].rearrange("p i (r c) -> p i r c", r=2)
            phi = pblk[:, 0:n, 1, :].rearrange("p i (r c) -> p i r c", r=2)[:, :, ::-1, :]
            self.tt(plo, plo, phi, ALU.add)
            self.tt(zs[:, lo:hi, :], zs[:, lo:hi, :], pblk[:, 0:n, 0, :], ALU.add)
            hi = lo
        Pt = [self.sc3[:, 0:2, :], self.av(PB, 128).rearrange("p (a c) -> p a c", a=2)]
        St = [self.sc3[:, 2, :], self.av(PB + 512, 64)]
        for i in range(2, NCH):
            k = i % 2
            xx = zs[:, i - 2, :].unsqueeze(1).to_broadcast([128, 2, 64])
            self.tt(Pt[k], m2c, xx, ALU.mult)
            self.tt(St[k].rearrange("p (r c) -> p r c", r=2), Pt[k][:, 0, :].rearrange("p (r c) -> p r c", r=2),
                    Pt[k][:, 1, :].rearrange("p (r c) -> p r c", r=2)[:, ::-1, :], ALU.add)
            self.tt(zs[:, i, :], zs[:, i, :], St[k], ALU.add)
        if l == 0 and s == 0:
            self.tap("zs", self.av(ZS, NCH * 64))
        for q in range(4):
            ysq = ys[0]
            for g8 in range(8):
                g = q * 8 + g8
                g2, gp = g // 2, g % 2
                if gp == 0:
                    rs = RING + (g2 % 3) * RSZ
                    gt = self.av(rs, 256, BF16).rearrange("p (a m) -> p a m", a=2)
                    mo = self.av(rs + 512, 512).rearrange("p (a b m) -> p a b m", a=2, b=2)
                    self.dma(gt, d["s5G_d"][l, 2 * g2:2 * g2 + 2].rearrange("g p m -> p g m"), key=f"s5y{g2 % 3}")
                    for pp in range(2):
                        for dd in range(2):
                            for ri in range(2):
                                self.dma(mo[pp * 64:(pp + 1) * 64, dd, ri, :], d["s5Mout_d"][l, dd, ri][:, 2 * g2 + pp, :],
                                         key=f"s5y{g2 % 3}")
                ps = self.bank()
                pr = slice(gp * 64, (gp + 1) * 64)
                self.mm(ps[:, 0:NCH], gt[:, gp, :], uall[:, g, :], start=True, stop=False)
                for ri in range(2):
                    self.mm(ps[:, 1:NCH], mo[pr, 0, ri, :], zs[pr, 0:NCH - 1, ri * 32 + g2], start=False, stop=False)
                for ri in range(2):
                    col = ri * 32 + 16 + g2
                    self.mm(ps[:, 30::-1], mo[pr, 1, ri, :], zs[pr, 0:31, col], start=False, stop=False)
                    self.mm(ps[:, NCH - 1:31:-1], mo[pr, 1, ri, :], zs[pr, 31:NCH - 1, col], start=False, stop=(ri == 1))
                self.cp(ysq[:, g8, :], ps[:, 0:NCH], eng=("act" if g8 % 2 else "dve"))
            for pc in range(5):
                ncw = 64 if pc < 4 else 32
                ps = self.bank()
                for j in range(8):
                    for g8 in range(8):
                        self.mm(ps[:, j:ncw * 8:8], self.bandb[:, j, 112 - 16 * g8:240 - 16 * g8],
                                ysq[:, g8, pc * 64:pc * 64 + ncw], start=(g8 == 0), stop=(g8 == 7))
                self.act(yg[:, q, pc * 512:pc * 512 + ncw * 8], ps[:, 0:ncw * 8], AF.Gelu_apprx_tanh)

        if self.limit < 7:
            return
        if l == 0 and s == 0:
            self.tap("yg", yg.rearrange("p k t -> p (k t)"))
        B = UALL
        wglu = self.av(B, 4 * 4 * 128, BF16).rearrange("p (f k c) -> p f k c", f=4, k=4); B += 4096
        for ft in range(4):
            self.dma(wglu[:, ft].rearrange("p k c -> p (k c)"), d["glu_r"][l, ft], key="wglu", q="pool")
        sg = [self.av(B + i * 1024, 512, BF16) for i in range(2)]; B += 2048
        n = 0
        for ft in range(4):
            for ci, (c0, cn) in enumerate(chunks):
                ps = self.bank()
                for kt in range(4):
                    self.mm(ps[:, 0:cn], wglu[:, ft, kt, :], yg[:, kt, c0:c0 + cn], start=(kt == 0), stop=(kt == 3))
                sgt = sg[n % 2]; n += 1
                self.act(sgt[:, 0:cn], ps[:, 0:cn], AF.Sigmoid, bias=pv[:, PV_GLUB + ft:PV_GLUB + ft + 1])
                self.tt(yall[:, ft, c0:c0 + cn], yg[:, ft, c0:c0 + cn], sgt[:, 0:cn], ALU.mult)
        if l == 0 and s == 0:
            self.tap("yall", yall.rearrange("p k t -> p (k t)"))
        wo = self.av(B, 8 * D, BF16).rearrange("p (k n) -> p k n", k=8); B += 16384
        for kt in range(8):
            self.dma(wo[:, kt, :], d["w_out_r"][l, kt], key="wo", q="pool")
        tiles = list(range(2, TT)) if last else list(range(TT))
        WD = (self.ASZ - NFT * D * 2) // 64 * 64
        wd = self.av(WD, NFT * D, BF16).rearrange("p (f n) -> p f n", f=NFT)
        for ft in range(NFT):
            self.dma(wd[:, ft, :], d["w_down_r"][l, ft], key="wd", q="pool")
        B = self.resid_ln(s, l, tiles, B, mod_gate=2, lnrow0=0, to_out=False, src_name=("xin" if l == 0 else "xs_d"), lim=WD,
                          mmfn=lambda ps2, t: [self.mm(ps2[:, h * 512:(h + 1) * 512], yall[:, kt, t * 128:(t + 1) * 128],
                                                       wo[:, kt, h * 512:(h + 1) * 512], start=(kt == 0), stop=(kt == 7))
                                               for h in range(2) for kt in range(8)])

        if self.limit < 8:
            return
        self.ffn(s, l, wd, WD)

    def resid_ln(self, s, l, tiles, B, mod_gate, lnrow0, to_out, mmfn, tile_off=0, src_name="xs_d", lim=None):
        d = self.dram
        gc = self.av(B, D); B += 4096
        gx = self.av(B, D); B += 4096
        lg = self.av(B, D); B += 4096
        lb = self.av(B, D); B += 4096
        xt = [self.av(B + i * 4096, D) for i in range(3)]; B += 12288
        vt = [self.av(B + i * 4096, D) for i in range(3)]; B += 12288
        assert B <= (self.ASZ if lim is None else lim), (B, self.ASZ, lim)
        ada = d["ada_d"]
        bc = lambda ap: ap.partition_broadcast(128).rearrange("p a n -> p (a n)")
        self.dma(gc, bc(ada[l, 2:3, mod_gate * D:(mod_gate + 1) * D]), key="bc0")
        self.dma(gx, bc(ada[l, s:s + 1, mod_gate * D:(mod_gate + 1) * D]), key="bc1")
        self.dma(lg, bc(d["lnrow"][l, lnrow0:lnrow0 + 1, :]), key="bc2")
        self.dma(lb, bc(d["lnrow"][l, lnrow0 + 1:lnrow0 + 2, :]), key="bc3")
        self.ts(gc, gc, 1.0 / ALPHA, None, ALU.mult)
        self.ts(gx, gx, 1.0 / ALPHA, None, ALU.mult)

        def finish(n, t, x, v):
            self.tt(v, v, lg, ALU.mult)
            self.tt(x, v, lb, ALU.add, eng="pool")
            if to_out:
                if t >= 2:
                    self.dma(d["out"][s, (t - 2) * 128:(t - 1) * 128, :], x, key=f"wx{n % 3}", q="pool")
            else:
                self.dma(d["xs_d"][s, t * 128:(t + 1) * 128, :], x, key=f"wx{n % 3}", q="pool")

        pending = None
        for n, t in enumerate(tiles):
            ps2 = self.bank(2)
            mmfn(ps2, t - tile_off)
            x = xt[n % 3]; v = vt[n % 3]
            self.dma(x, d[src_name][s, t * 128:(t + 1) * 128, :], key=f"rx{n % 3}")
            g = gc if t < 2 else gx
            self.tt(v, ps2, g, ALU.mult)
            self.tt(v, v, x, ALU.add)
            rstd, nmr = self.ln_stats(v, n % 4, eps=EPS / (ALPHA * ALPHA))
            self.act(v, v, AF.Identity, bias=nmr, scale=rstd)
            if pending is not None:
                finish(*pending)
            pending = (n, t, x, v)
        if pending is not None:
            finish(*pending)
        return B

    def ffn(self, s, l, wd, WD):
        d = self.dram
        last = (l == DEPTH - 1)
        slot = l % 2
        pv = self.pv[:, slot, :]
        PV_FCW = 48; PV_FCB = 246
        for ch in range(2):
            if ch == 0:
                ltiles = list(range(0, 10)); own = list(range(0, 9)); r0, r1 = 0, 14
                if last:
                    own = list(range(2, 9)); ltiles = list(range(2, 10))
            else:
                ltiles = list(range(8, 18)); own = list(range(9, 18)); r0, r1 = 14, 32
            t0 = ltiles[0]
            NTK = 1280
            B = 0
            h2T = self.av(B, 8 * NTK, BF16).rearrange("p (k t) -> p k t", k=8); B += 8 * NTK * 2
            hid = self.av(B, NFT * 1152, BF16).rearrange("p (f t) -> p f t", f=NFT); B += NFT * 1152 * 2
            wu = [self.av(B + i * 4096, 2048, BF16).rearrange("p (k c) -> p k c", k=8) for i in range(3)]; B += 12288
            R = r1 - r0
            GP = 66
            ug = self.av(B, (R + 2) * GP, BF16); B += ((R + 2) * GP * 2 + 63) // 64 * 64
            uc = self.av(B, 258, BF16); B += 576
            dgs = [self.av(B + i * 2304, 9 * 128, BF16).rearrange("p (k m) -> p k m", k=9) for i in range(2)]; B += 4608
            gl = self.av(B, 1152, BF16); B += 2304
            TB = B
            self.ln_mod_T(s, l, ltiles, h2T, 0, mod_shift=3, mod_scale=4, A0=TB)
            ug3 = ug.rearrange("p (r c) -> p r c", c=GP)
            self.memset(ug, 0.0)
            self.memset(uc, 0.0)
            loc = lambda r: 256 + 64 * r - t0 * 128
            ra = max(r0 - 1, 0); rb = min(r1 + 1, 32)
            nown = len(own) * 128
            own0 = own[0] * 128 - t0 * 128
            has_ctx = (ch == 0 and not last)
            lo = 256 if has_ctx else 0
            for ft in range(NFT):
                w = wu[ft % 3]
                self.dma(w.rearrange("p k c -> p (k c)"), d["w_up_r"][l, ft], key=f"wu{ft % 3}", q="pool")
                cw = lambda k: pv[:, PV_FCW + ft * 9 + k:PV_FCW + ft * 9 + k + 1]
                cb = pv[:, PV_FCB + ft:PV_FCB + ft + 1]
                dg = dgs[ft % 2]
                for k in range(9):
                    self.ts(dg[:, k, :], self.identb[:], cw(k), None, ALU.mult)
                r = ra
                while r < rb:
                    nr = min(8, rb - r)
                    ps = self.bank()
                    for kt in range(8):
                        self.mm(ps[:, 0:nr * 64], w[:, kt, 0:128], h2T[:, kt, loc(r):loc(r) + nr * 64],
                                start=(kt == 0), stop=(kt == 7))
                    gr = r - (r0 - 1)
                    self.cp(ug3[:, gr:gr + nr, 1:65], ps[:, 0:nr * 64].rearrange("p (r c) -> p r c", c=64), eng="act")
                    r += nr
                rr = 0
                while rr < R:
                    nr = min(8, R - rr)
                    ps = self.bank()
                    po = ps[:, 0:nr * 64].rearrange("p (r c) -> p r c", c=64)
                    for k in range(9):
                        di, dj = k // 3, k % 3
                        self.mm(po, dg[:, k, :], ug3[:, rr + di:rr + di + nr, dj:dj + 64], start=(k == 0), stop=(k == 8))
                    self.act(gl[:, lo + rr * 64:lo + (rr + nr) * 64], ps[:, 0:nr * 64], AF.Gelu_apprx_tanh, bias=cb)
                    rr += nr
                if has_ctx:
                    ps = self.bank()
                    for kt in range(8):
                        self.mm(ps[:, 0:256], w[:, kt, 0:128], h2T[:, kt, 0:256], start=(kt == 0), stop=(kt == 7))
                    self.cp(uc[:, 1:257], ps[:, 0:256], eng="act")
                    ps = self.bank()
                    for k in range(3):
                        self.mm(ps[:, 0:256], dg[:, 3 + k, :], uc[:, k:k + 256], start=(k == 0), stop=(k == 2))
                    self.act(gl[:, 0:256], ps[:, 0:256], AF.Gelu_apprx_tanh, bias=cb)
                c = 0
                while c < nown:
                    cn = min(512, nown - c)
                    ps = self.bank()
                    for kt in range(8):
                        self.mm(ps[:, 0:cn], w[:, kt, 128:256], h2T[:, kt, own0 + c:own0 + c + cn],
                                start=(kt == 0), stop=(kt == 7))
                    self.tt(hid[:, ft, c:c + cn], ps[:, 0:cn], gl[:, c:c + cn], ALU.mult)
                    c += cn
                if l == 0 and s == 0 and ch == 0 and ft == 0:
                    self.tap("hid0", hid[:, 0, :])
            self.resid_ln(s, l, own, TB, mod_gate=5, lnrow0=2, to_out=last, tile_off=own[0], lim=WD,
                          mmfn=lambda ps2, ti: [self.mm(ps2[:, h * 512:(h + 1) * 512], hid[:, ft, ti * 128:(ti + 1) * 128],
                                                        wd[:, ft, h * 512:(h + 1) * 512], start=(ft == 0), stop=(ft == NFT - 1))
                                                for h in range(2) for ft in range(NFT)])


def make_consts():
    c = np.zeros((128, NCONST), np.float32)
    c[:, C_ID:C_ID + 128] = np.eye(128, dtype=np.float32)
    k = np.arange(128)
    s_of = k // 16
    c[:, C_MF:C_MF + 128] = (s_of[None, :] >= s_of[:, None]).astype(np.float32)
    c[:, C_MB:C_MB + 128] = (s_of[:, None] >= s_of[None, :]).astype(np.float32)
    for g8 in range(8):
        band = np.zeros((128, 240), np.float32)
        for kk in range(16 * g8, 16 * g8 + 16):
            band[kk, kk - 16 * g8 + 112] = 1.0
        c[:, C_BAND + g8 * 240:C_BAND + (g8 + 1) * 240] = band
    c[:, C_EV:C_EV + 16] = np.arange(-7, 9, dtype=np.float32)[None, :]
    c[0:64, C_SPM] = 1.0; c[64:128, C_SPM] = -1.0
    c[0:64, C_SMP] = -1.0; c[64:128, C_SMP] = 1.0
    return c


def pack_shared(inp, L=DEPTH):
    f = lambda a: np.ascontiguousarray(np.asarray(a, dtype=np.float32)[:L])
    sh = {}
    sh["w_mod"] = f(inp["w_mod"])
    sh["b_mod"] = f(inp["b_mod"])
    w_in = f(inp["w_in"])
    sh["w_in_r"] = f(w_in.reshape(L, 8, 128, 12, 128).transpose(0, 3, 2, 1, 4).reshape(L, 12, 128, 1024))
    lam_re = f(inp["s5_lam_re"]); lam_im = f(inp["s5_lam_im"])
    b_re = f(inp["s5_b_re"]); b_im = f(inp["s5_b_im"]); c_re = f(inp["s5_c_re"]); c_im = f(inp["s5_c_im"])
    pack = np.zeros((L, 2, 128, 2112), np.float32)
    lrT = lam_re.transpose(0, 1, 3, 2)
    liT = lam_im.transpose(0, 1, 3, 2)
    pack[:, :, 0:64, 0:32] = lrT; pack[:, :, 64:128, 0:32] = lrT
    pack[:, :, 0:64, 32:64] = liT; pack[:, :, 64:128, 32:64] = liT
    brT = b_re.transpose(0, 1, 3, 2, 4).reshape(L, 2, 64, 512)
    biT = b_im.transpose(0, 1, 3, 2, 4).reshape(L, 2, 64, 512)
    crT = c_re.transpose(0, 1, 4, 2, 3).reshape(L, 2, 64, 512)
    ciT = c_im.transpose(0, 1, 4, 2, 3).reshape(L, 2, 64, 512)
    pack[:, :, 0:64, 64:576] = brT; pack[:, :, 64:128, 64:576] = biT
    pack[:, :, 0:64, 576:1088] = biT; pack[:, :, 64:128, 576:1088] = brT
    pack[:, :, 0:64, 1088:1600] = crT; pack[:, :, 64:128, 1088:1600] = ciT
    pack[:, :, 0:64, 1600:2112] = ciT; pack[:, :, 64:128, 1600:2112] = crT
    sh["s5pack"] = pack
    dd = f(inp["s5_d"]).reshape(L, 32, 16).transpose(0, 2, 1)
    sh["s5dd"] = f(np.tile(dd, (1, 8, 1)))
    sh["log_dt"] = f(inp["s5_log_dt"])
    glu = f(inp["s5_w_glu"])
    sh["glu_r"] = f(glu.reshape(L, 4, 128, 4, 128).transpose(0, 3, 2, 1, 4).reshape(L, 4, 128, 512))
    wa = f(inp["lru_w_a"]); wx = f(inp["lru_w_x"])
    gate = np.zeros((L, 4, 128, 2, 2, 128), np.float32)
    for q in range(4):
        for hh in range(2):
            h = 2 * q + hh
            gate[:, q, hh * 64:(hh + 1) * 64, :, 0, hh * 64:(hh + 1) * 64] = wa[:, :, h].transpose(0, 2, 1, 3)
            gate[:, q, hh * 64:(hh + 1) * 64, :, 1, hh * 64:(hh + 1) * 64] = wx[:, :, h].transpose(0, 2, 1, 3)
    sh["gate_r"] = f(gate.reshape(L, 4, 128, 512))
    sh["w_out_r"] = f(f(inp["w_out"]).reshape(L, 8, 128, D))
    wup = f(inp["ffn_w_up"])
    u = wup[:, :, :DFF].reshape(L, 8, 128, NFT, 128)
    v = wup[:, :, DFF:].reshape(L, 8, 128, NFT, 128)
    uv = np.concatenate([u, v], axis=-1)
    sh["w_up_r"] = f(uv.transpose(0, 3, 2, 1, 4).reshape(L, NFT, 128, 2048))
    sh["w_down_r"] = f(f(inp["ffn_w_down"]).reshape(L, NFT, 128, D))
    pvec = np.zeros((L, 128, NPV), np.float32)
    tp = lambda a, n: f(a).reshape(L, n, 128).transpose(0, 2, 1)
    pvec[:, :, 0:4] = tp(inp["s5_b_glu"], 4)
    cw = f(inp["lru_conv_w"]).reshape(L, 4, 4, 128)
    pvec[:, :, 4:20] = cw.transpose(0, 3, 2, 1).reshape(L, 128, 16)
    pvec[:, :, 20:24] = tp(inp["lru_conv_b"], 4)
    pvec[:, :, 24:32] = f(inp["lru_b_a"]).reshape(L, 2, 4, 128).transpose(0, 3, 1, 2).reshape(L, 128, 8)
    pvec[:, :, 32:40] = f(inp["lru_b_x"]).reshape(L, 2, 4, 128).transpose(0, 3, 1, 2).reshape(L, 128, 8)
    pvec[:, :, 40:48] = f(inp["lru_lam"]).reshape(L, 2, 4, 128).transpose(0, 3, 1, 2).reshape(L, 128, 8)
    fcw = f(inp["ffn_conv_w"]).reshape(L, 9, NFT, 128)
    pvec[:, :, 48:246] = fcw.transpose(0, 3, 2, 1).reshape(L, 128, NFT * 9)
    pvec[:, :, 246:268] = tp(inp["ffn_conv_b"], NFT)
    sh["pvec"] = pvec
    sh["lnrow"] = f(np.stack([f(inp["ln1_g"]), f(inp["ln1_b"]), f(inp["ln2_g"]), f(inp["ln2_b"])], axis=1))
    sh["consts"] = make_consts()
    return sh


def pack_core(inp, core):
    f = lambda a: np.ascontiguousarray(np.asarray(a, dtype=np.float32))
    b0 = 2 * core
    x = f(inp["x"][b0:b0 + 2]); ctx = f(inp["ctx"][b0:b0 + 2])
    xin = np.concatenate([ctx, x], axis=1)
    conds = np.stack([f(inp["c"][b0]), f(inp["c"][b0 + 1]), f(inp["c_ctx"])], axis=0)
    condT = conds.reshape(3, 8, 128).transpose(2, 1, 0).reshape(128, 24)
    return {"xin": f(xin), "condT": f(condT)}


_CACHE = {}


def kernel(**inputs):
    if "nc" not in _CACHE:
        _CACHE["nc"] = Builder().build()
    nc = _CACHE["nc"]
    sh = pack_shared(inputs)
    in_maps = []
    for c in range(NCORES):
        m = dict(sh)
        m.update(pack_core(inputs, c))
        in_maps.append(m)
    res = run_bass_kernel_spmd(nc, in_maps, core_ids=list(range(NCORES)))
    outs = [np.asarray(r["out"], dtype=np.float32) for r in res.results]
    return np.concatenate(outs, axis=0).reshape(16, 2048, D)
```

```python
import math
import os
import numpy as np
SKIP = os.environ.get('K_SKIP', '').split(',')
import concourse.bass as bass
import concourse.mybir as mybir
from concourse.bass_utils import run_bass_kernel_spmd

F32 = mybir.dt.float32
BF16 = mybir.dt.bfloat16
AF = mybir.ActivationFunctionType
ALU = mybir.AluOpType

NCORES = 8
DEPTH = 4
D = 1024
NT = 2304
NCTX = 256
TT = 18
NCH = 288
DFF = 2816
NFT = 22
ALPHA = 8.0 ** 0.25
EPS = 1e-6
NPV = 268
C_ID = 0; C_MF = 128; C_MB = 256; C_BAND = 384; C_EV = 384 + 8 * 240; C_SPM = C_EV + 16; C_SMP = C_SPM + 1
NCONST = C_SMP + 1
ESZ = {F32: 4, BF16: 2}


class Op:
    __slots__ = ("eng", "fn", "deps", "ddeps", "sig", "dma", "key", "semval", "idx")


class Prog:
    EPOCH = 6000

    def __init__(self, nc):
        self.nc = nc
        self.ops = []
        self.recs = {}
        self.dma_keys = {}

    @staticmethod
    def region(ap):
        t = ap.tensor
        name = t.name
        esz = ESZ[ap.dtype]
        pairs = ap.ap
        off = int(ap.offset)
        if str(ap.space) == "DRAM":
            lo = hi = off
            for st, cnt in pairs:
                if st >= 0:
                    hi += st * (cnt - 1)
                else:
                    lo += st * (cnt - 1)
            return (name, 0, 1, lo * esz, (hi + 1) * esz)
        row = 1
        for x in t.shape[1:]:
            row *= x
        p0 = off // row
        lo = hi = off - p0 * row
        pcnt = pairs[0][1]
        for st, cnt in pairs[1:]:
            if st >= 0:
                hi += st * (cnt - 1)
            else:
                lo += st * (cnt - 1)
        return (name, p0, p0 + pcnt, lo * esz, (hi + 1) * esz)

    def add(self, eng, fn, reads=(), writes=(), dma=False, key=None):
        op = Op()
        op.eng = eng; op.fn = fn; op.dma = dma; op.key = key; op.sig = False; op.semval = None
        op.idx = len(self.ops)
        deps = set()
        for ap in reads:
            name, p0, p1, b0, b1 = self.region(ap)
            lst = self.recs.setdefault(name, [])
            for r in lst:
                if r[5] and r[4] != op.idx and r[0] < p1 and p0 < r[1] and r[2] < b1 and b0 < r[3]:
                    deps.add(r[4])
            for r in lst:
                if r[4] == op.idx:
                    if (not r[5]) and r[0] == p0 and r[1] == p1 and r[2] == b0 and r[3] == b1:
                        break
                    continue
                if (not r[5]) and r[0] == p0 and r[1] == p1 and r[2] == b0 and r[3] == b1 \
                        and self.ops[r[4]].eng == eng and not self.ops[r[4]].dma and not dma:
                    r[4] = op.idx
                    break
            else:
                lst.append([p0, p1, b0, b1, op.idx, False])
        for ap in writes:
            name, p0, p1, b0, b1 = self.region(ap)
            lst = self.recs.setdefault(name, [])
            keep = []
            for r in lst:
                ov = r[0] < p1 and p0 < r[1] and r[2] < b1 and b0 < r[3]
                if ov and r[4] != op.idx:
                    deps.add(r[4])
                cov = p0 <= r[0] and r[1] <= p1 and b0 <= r[2] and r[3] <= b1
                if not (ov and cov):
                    keep.append(r)
            keep.append([p0, p1, b0, b1, op.idx, True])
            self.recs[name] = keep
        fd = []
        dd = {}
        for di in deps:
            d = self.ops[di]
            if d.dma:
                dd[d.key] = self.dma_keys[d.key] * 16
            elif dma or d.eng != eng:
                fd.append(di)
            elif eng != "pe":
                fd.append(di)
        op.deps = fd
        op.ddeps = dd
        if dma:
            c = self.dma_keys.get(key, 0) + 1
            self.dma_keys[key] = c
            op.semval = c * 16
        self.ops.append(op)
        return op

    def emit(self):
        nc = self.nc
        ops = self.ops
        for op in ops:
            best = {}
            for di in op.deps:
                d = ops[di]
                if d.eng not in best or best[d.eng] < di:
                    best[d.eng] = di
            op.deps = list(best.values())
            for di in op.deps:
                ops[di].sig = True
        cnt = {}
        sems = {}
        for op in ops:
            if op.dma or not op.sig:
                continue
            c = cnt.get(op.eng, 0)
            ep = c // self.EPOCH
            k = (op.eng, ep)
            if k not in sems:
                sems[k] = nc.alloc_semaphore(f"s_{op.eng}_{ep}")
            op.semval = (sems[k], c - ep * self.EPOCH + 1)
            cnt[op.eng] = c + 1
        dsems = {}
        for key in self.dma_keys:
            dsems[key] = nc.alloc_semaphore(f"d_{key}")
        for op in ops:
            if op.dma:
                op.semval = (dsems[op.key], op.semval)
        self.nsem = len(sems) + len(dsems)
        engs = {"pe": [], "act": [], "dve": [], "pool": [], "sp": []}
        for op in ops:
            engs[op.eng].append(op)

        def run(eobj, lst):
            known = {}
            for op in lst:
                for di in op.deps:
                    sem, val = ops[di].semval
                    if known.get(sem.name, 0) < val:
                        eobj.wait_ge(sem, val)
                        known[sem.name] = val
                for key, val in op.ddeps.items():
                    sem = dsems[key]
                    if known.get(sem.name, 0) < val:
                        eobj.wait_ge(sem, val)
                        known[sem.name] = val
                ins = op.fn(eobj)
                if op.dma:
                    ins.then_inc(op.semval[0], 16)
                elif op.sig:
                    ins.then_inc(op.semval[0], 1)
            return known

        with nc.Block() as block:
            @block.tensor
            def _(e):
                run(e, engs["pe"])

            @block.scalar
            def _(e):
                run(e, engs["act"])

            @block.vector
            def _(e):
                run(e, engs["dve"])

            @block.gpsimd
            def _(e):
                run(e, engs["pool"])

            @block.sync
            def _(e):
                known = run(e, engs["sp"])
                for key, c in self.dma_keys.items():
                    sem = dsems[key]
                    if known.get(sem.name, 0) < c * 16:
                        e.wait_ge(sem, c * 16)


class Builder:
    def __init__(self, nlayers=DEPTH, nseq=2, taps=None, limit=99):
        self.limit = limit
        self.nlayers = nlayers
        self.nseq = nseq
        self.taps = taps or []
        nc = bass.Bass("TRN2", target_bir_lowering=False)
        self.nc = nc
        self.P = Prog(nc)
        self.psn = 0
        self.kpn = 0
        self.dram = {}
        self._declare()
        self._alloc()

    def din(self, name, shape, dt=F32):
        self.dram[name] = self.nc.dram_tensor(name, list(shape), dt, kind="ExternalInput").ap()
        return self.dram[name]

    def dscr(self, name, shape, dt=F32, kind="Internal"):
        self.dram[name] = self.nc.dram_tensor(name, list(shape), dt, kind=kind).ap()
        return self.dram[name]

    def _declare(self):
        L = self.nlayers
        self.din("xin", [2, NT, D])
        self.din("condT", [128, 24])
        self.din("w_mod", [L, D, 6 * D])
        self.din("b_mod", [L, 6 * D])
        self.din("w_in_r", [L, 12, 128, 8 * 128])
        self.din("s5pack", [L, 2, 128, 2112])
        self.din("s5dd", [L, 128, 32])
        self.din("log_dt", [L, 2, 32])
        self.din("glu_r", [L, 4, 128, 4 * 128])
        self.din("gate_r", [L, 4, 128, 4 * 128])
        self.din("w_out_r", [L, 8, 128, D])
        self.din("w_up_r", [L, NFT, 128, 8 * 256])
        self.din("w_down_r", [L, NFT, 128, D])
        self.din("pvec", [L, 128, NPV])
        self.din("lnrow", [L, 4, D])
        self.din("consts", [128, NCONST])
        self.dscr("out", [2, 2048, D], kind="ExternalOutput")
        self.dscr("xs_d", [2, NT, D])
        self.dscr("ada_d", [L, 3, 6 * D])
        self.dscr("s5G_d", [L, 32, 128, 128], BF16)
        self.dscr("s5Win_d", [L, 2, 32, 128, 128], BF16)
        self.dscr("s5Mout_d", [L, 2, 2, 64, 32, 128])
        self.dscr("s5mu_d", [L, 2, 2, 64, 32])
        for name, shape, dt in self.taps:
            self.dscr("tap_" + name, shape, dt, kind="ExternalOutput")

    def _alloc(self):
        nc = self.nc
        self.cst = nc.alloc_sbuf_tensor("cst", [128, NCONST], F32)
        self.identb = nc.alloc_sbuf_tensor("identb", [128, 128], BF16)
        self.bandb = nc.alloc_sbuf_tensor("bandb", [128, 8, 240], BF16)
        self.adaT = nc.alloc_sbuf_tensor("adaT", [128, DEPTH, 48, 3], F32)
        self.pv = nc.alloc_sbuf_tensor("pv", [128, 2, NPV], F32)
        self.pv2 = nc.alloc_sbuf_tensor("pv2", [128, 2, 32], F32)
        self.m12 = nc.alloc_sbuf_tensor("m12", [128, 2, 64], F32)
        self.small = nc.alloc_sbuf_tensor("small", [128, 64], F32)
        self.sc3 = nc.alloc_sbuf_tensor("sc3", [128, 3, 64], F32)
        rem = nc.sbuf_bytes_remaining
        self.ASZ = (rem - 2048) // 64 * 64
        self.arena = nc.alloc_sbuf_tensor("arena", [128, self.ASZ // 4], F32)
        self.ps = nc.alloc_psum_tensor("ps", [128, 4096], F32)

    def av(self, off, n, dt=F32):
        assert off % 4 == 0 and off + n * ESZ[dt] <= self.ASZ, (off, n, dt, self.ASZ)
        a = self.arena[:, off // 4:(off + n * ESZ[dt] + 3) // 4]
        if dt == BF16:
            a = a.bitcast(BF16)
            a = a[:, 0:n]
        return a

    def bank(self, n=1):
        if n == 2 and self.psn % 2 == 1:
            self.psn += 1
        b = self.psn % 8
        self.psn += n
        return self.ps[:, b * 512:(b + n) * 512]

    def mm(self, out, lhsT, rhs, start=True, stop=True):
        self.P.add("pe", lambda e: e.matmul(out, lhsT, rhs, start=start, stop=stop),
                   reads=[lhsT, rhs], writes=[out])

    def tr(self, out, in_, ident):
        self.P.add("pe", lambda e: e.transpose(out, in_, ident), reads=[in_, ident], writes=[out])

    def act(self, out, in_, func, bias=0.0, scale=1.0, eng="act"):
        rd = [in_]
        if not isinstance(bias, (int, float)):
            rd.append(bias)
        if not isinstance(scale, (int, float)):
            rd.append(scale)
        self.P.add("act", lambda e: e.activation(out=out, in_=in_, func=func, bias=bias, scale=scale),
                   reads=rd, writes=[out])

    def tt(self, out, in0, in1, op, eng="dve"):
        self.P.add(eng, lambda e: e.tensor_tensor(out, in0, in1, op), reads=[in0, in1], writes=[out])

    def ts(self, out, in0, s1, s2, op0, op1=None, eng="dve"):
        rd = [in0]
        if not isinstance(s1, (int, float)):
            rd.append(s1)
        if s2 is not None and not isinstance(s2, (int, float)):
            rd.append(s2)
        if op1 is None:
            self.P.add(eng, lambda e: e.tensor_scalar(out, in0, s1, None, op0), reads=rd, writes=[out])
        else:
            self.P.add(eng, lambda e: e.tensor_scalar(out, in0, s1, s2, op0, op1), reads=rd, writes=[out])

    def stt(self, out, in0, scalar, in1, op0, op1, eng="dve"):
        rd = [in0, in1]
        if not isinstance(scalar, (int, float)):
            rd.append(scalar)
        self.P.add(eng, lambda e: e.scalar_tensor_tensor(out, in0, scalar, in1, op0, op1), reads=rd, writes=[out])

    def cp(self, out, in_, eng="dve"):
        if eng == "act":
            self.P.add("act", lambda e: e.copy(out, in_), reads=[in_], writes=[out])
        else:
            self.P.add(eng, lambda e: e.tensor_copy(out, in_), reads=[in_], writes=[out])

    def memset(self, out, val, eng="dve"):
        self.P.add(eng, lambda e: e.memset(out, val), reads=[], writes=[out])

    def dma(self, out, in_, key, q="sp", slow=False):
        if slow:
            fn = lambda e: e.dma_start(out=out, in_=in_, allow_slow_non_contiguous=True)
        else:
            fn = lambda e: e.dma_start(out=out, in_=in_)
        self.P.add(q, fn, reads=[in_], writes=[out], dma=True, key=key)

    def tap(self, name, sb_ap):
        if ("tap_" + name) in self.dram:
            self.dma(self.dram["tap_" + name], sb_ap, key="tap")

    def build(self):
        self.load_consts()
        if self.limit >= 1:
            self.prologue_ada()
        if self.limit >= 2:
            for l in range(self.nlayers):
                self.prologue_s5(l)
        if self.limit >= 3:
            for s in range(self.nseq):
                for l in range(self.nlayers):
                    self.layer(s, l)
        if "tap_xs" in self.dram:
            for s_ in range(2):
                for t_ in range(TT):
                    self.dma(self.dram["tap_xs"][s_, t_ * 128:(t_ + 1) * 128, :], self.dram["xs_d"][s_, t_ * 128:(t_ + 1) * 128, :], key="tap")
        self.P.emit()
        return self.nc

    def load_consts(self):
        d = self.dram
        self.dma(self.cst[:], d["consts"], key="cst")
        self.cp(self.identb[:], self.cst[:, C_ID:C_ID + 128])
        self.cp(self.bandb[:].rearrange("p a b -> p (a b)"), self.cst[:, C_BAND:C_BAND + 1920])

    def prologue_ada(self):
        d = self.dram
        A = 0
        condT = self.av(A, 24); A += 96
        siluT = self.av(A, 24, BF16); A += 64
        bm3 = self.av(A, 6 * D); A += 6 * D * 4
        adarow = self.av(A, 6 * D); A += 6 * D * 4
        wm = [self.av(A + i * 8192, 4096, BF16) for i in range(2)]; A += 16384
        self.dma(condT, d["condT"], key="condT")
        self.act(siluT, condT, AF.Silu)
        siluT3 = siluT.rearrange("p (k c) -> p k c", c=3)
        ident = self.cst[:, C_ID:C_ID + 128]
        for l in range(self.nlayers):
            self.dma(bm3[0:3, :], d["b_mod"][l:l + 1, :].partition_broadcast(3).rearrange("p a n -> p (a n)"), key="bm3")
            for cc in range(12):
                w = wm[cc % 2]
                w3 = w.rearrange("p (k n) -> p k n", k=8)
                self.dma(w3, d["w_mod"][l][:, cc * 512:(cc + 1) * 512].rearrange("(k p) n -> p k n", p=128),
                         key=f"wm{cc % 2}", q="pool")
                ps = self.bank()
                for kt in range(8):
                    self.mm(ps[0:3, :], siluT3[:, kt, :], w3[:, kt, :], start=(kt == 0), stop=(kt == 7))
                self.tt(adarow[0:3, cc * 512:(cc + 1) * 512], ps[0:3, :], bm3[0:3, cc * 512:(cc + 1) * 512], ALU.add)
            self.dma(d["ada_d"][l], adarow[0:3, :], key="ada_st")
            ps = self.bank()
            for j in range(48):
                self.tr(ps[:, j * 3:(j + 1) * 3], adarow[0:3, j * 128:(j + 1) * 128], ident[0:3, 0:3])
            aT = self.adaT[:, l].rearrange("p a c -> p (a c)")
            self.cp(aT, ps[:, 0:144])
            for mod in (1, 4):
                v = self.adaT[:, l, mod * 8:(mod + 1) * 8, :].rearrange("p a c -> p (a c)")
                self.ts(v, v, 1.0, None, ALU.add)

    def prologue_s5(self, l):
        d = self.dram
        cst = self.cst
        sgn_pm = cst[:, C_SPM:C_SPM + 1]
        sgn_mp = cst[:, C_SMP:C_SMP + 1]
        EV = cst[:, C_EV:C_EV + 16]
        identf = cst[:, C_ID:C_ID + 128]
        A = [0]

        def al(n, dt=F32):
            v = self.av(A[0], n, dt)
            A[0] += (n * ESZ[dt] + 63) // 64 * 64
            return v

        Q = [al(4096), al(4096)]
        CF = [al(4096), al(4096)]
        SP = al(2112)
        DTb = al(32); dtt = al(32); Aa = al(32); PHI = al(32)
        MAG = al(512); ANG = al(512); TMP = al(512); SN = al(512); CS = al(512); LRe = al(512); LIe = al(512)
        t32 = [al(32) for _ in range(8)]
        P1 = al(512); P2 = al(512); P1n = al(512); P2n = al(512); C2s = al(512); C1pm = al(512); C2n = al(512)
        T1 = al(4096); T2 = al(4096); W = al(4096)
        Wb = al(4096, BF16)
        Gs = al(512); Gs2 = al(512)
        Gb = al(4096, BF16)
        Dd = al(32)
        v3 = lambda x: x.rearrange("p (g c) -> p g c", g=32)
        v4 = lambda x: x.rearrange("p (g s c) -> p g s c", g=32, s=8)
        TWO_PI = 2.0 * math.pi
        for dd in range(2):
            self.dma(SP, d["s5pack"][l, dd], key="s5pack")
            self.dma(DTb, d["log_dt"][l, dd:dd + 1, :].partition_broadcast(128).rearrange("p a n -> p (a n)"), key="s5dt")
            LR = SP[:, 0:32]; LI = SP[:, 32:64]
            B1 = SP[:, 64:576]; B2 = SP[:, 576:1088]; C1 = SP[:, 1088:1600]; C2 = SP[:, 1600:2112]
            self.act(dtt, DTb, AF.Exp)
            self.tt(Aa, LR, dtt, ALU.mult)
            self.tt(PHI, LI, dtt, ALU.mult)
            bc_ge = lambda x: x.unsqueeze(2).to_broadcast([128, 32, 16])
            ev_b = EV.unsqueeze(1).to_broadcast([128, 32, 16])
            self.tt(v3(MAG), bc_ge(Aa), ev_b, ALU.mult)
            self.act(MAG, MAG, AF.Exp)
            self.tt(v3(ANG), bc_ge(PHI), ev_b, ALU.mult)
            MAGIC = 12582912.0
            self.ts(TMP, ANG, 1.0 / TWO_PI, MAGIC, ALU.mult, ALU.add)
            self.ts(TMP, TMP, -MAGIC, None, ALU.add)
            self.stt(TMP, TMP, -TWO_PI, ANG, ALU.mult, ALU.add)
            self.act(SN, TMP, AF.Sin)
            self.ts(ANG, ANG, 0.5 * math.pi, None, ALU.add)
            self.ts(TMP, ANG, 1.0 / TWO_PI, MAGIC, ALU.mult, ALU.add)
            self.ts(TMP, TMP, -MAGIC, None, ALU.add)
            self.stt(TMP, TMP, -TWO_PI, ANG, ALU.mult, ALU.add)
            self.act(CS, TMP, AF.Sin)
            self.tt(LRe, MAG, CS, ALU.mult)
            self.tt(LIe, MAG, SN, ALU.mult)
            LRe3 = v3(LRe); LIe3 = v3(LIe)
            nr, den, rden, kr, ki, u1, u2, krs = t32
            self.ts(nr, LRe3[:, :, 8], -1.0, None, ALU.add)
            l1i = LIe3[:, :, 8]
            self.tt(den, LR, LR, ALU.mult)
            self.tt(u1, LI, LI, ALU.mult)
            self.tt(den, den, u1, ALU.add)
            self.P.add("dve", lambda e, o=rden, i=den: e.reciprocal(o, i), reads=[den], writes=[rden])
            self.tt(u1, nr, LR, ALU.mult)
            self.tt(u2, l1i, LI, ALU.mult)
            self.tt(u1, u1, u2, ALU.add)
            self.tt(kr, u1, rden, ALU.mult)
            self.tt(u1, l1i, LR, ALU.mult)
            self.tt(u2, nr, LI, ALU.mult)
            self.tt(u1, u1, u2, ALU.subtract)
            self.tt(ki, u1, rden, ALU.mult)
            kis = u1
            self.ts(kis, ki, sgn_mp, None, ALU.mult)
            self.ts(krs, kr, sgn_mp, None, ALU.mult)
            self.tt(v3(P1), bc_ge(kr), v3(B1), ALU.mult)
            self.tt(v3(TMP), bc_ge(kis), v3(B2), ALU.mult)
            self.tt(P1, P1, TMP, ALU.add)
            self.tt(v3(P2), bc_ge(krs), v3(B2), ALU.mult)
            self.tt(v3(TMP), bc_ge(ki), v3(B1), ALU.mult)
            self.tt(P2, P2, TMP, ALU.subtract)
            self.ts(P1n, P1, sgn_pm, None, ALU.mult)
            self.ts(P2n, P2, sgn_pm, None, ALU.mult)
            self.ts(C2s, C2, sgn_mp, None, ALU.mult)
            self.ts(C1pm, C1, sgn_pm, None, ALU.mult)
            self.ts(C2n, C2, -1.0, None, ALU.mult)

            def esl(tab3, e0, step):
                if step > 0:
                    return tab3[:, :, e0:e0 + 8]
                stop = e0 - 8
                return tab3[:, :, e0:(stop if stop >= 0 else None):-1]

            def build(dst, e0, step, PA, PB):
                lr = esl(LRe3, e0, step).unsqueeze(3).to_broadcast([128, 32, 8, 16])
                li = esl(LIe3, e0, step).unsqueeze(3).to_broadcast([128, 32, 8, 16])
                pa = v3(PA).unsqueeze(2).to_broadcast([128, 32, 8, 16])
                pb = v3(PB).unsqueeze(2).to_broadcast([128, 32, 8, 16])
                self.tt(v4(T1), lr, pa, ALU.mult)
                self.tt(v4(T2), li, pb, ALU.mult)
                self.tt(dst, T1, T2, ALU.add)

            if dd == 0:
                build(W, 14, -1, P1, P2)
            else:
                build(W, 7, +1, P1, P2)
            W3 = W.rearrange("p (g m) -> p g m", g=32)
            Wb3 = Wb.rearrange("p (g m) -> p g m", g=32)
            for g4 in range(8):
                ps = self.bank()
                for k in range(4):
                    self.tr(ps[:, k * 128:(k + 1) * 128], W3[:, g4 * 4 + k, :], identf)
                self.cp(Wb[:, g4 * 512:(g4 + 1) * 512], ps, eng=("act" if g4 % 2 else "dve"))
            self.dma(d["s5Win_d"][l, dd].rearrange("g p m -> p g m"), Wb3, key="s5st")
            if dd == 0:
                build(Q[dd], 7, -1, P1n, P2n)
                build(CF[dd], 7, +1, C1, C2s)
            else:
                build(Q[dd], 7, +1, P1n, P2n)
                build(CF[dd], 7, -1, C1, C2s)
            if dd == 0:
                build(W, 8, +1, C1pm, C2n)
            else:
                build(W, 15, -1, C1pm, C2n)
            for ri in range(2):
                self.dma(d["s5Mout_d"][l, dd, ri].rearrange("p g m -> p g m"), W3[ri * 64:(ri + 1) * 64, :, :], key="s5st")
            self.dma(d["s5mu_d"][l, dd, 0], LRe3[0:64, :, 15], key="s5st", slow=True)
            self.dma(d["s5mu_d"][l, dd, 1], LIe3[0:64, :, 15], key="s5st", slow=True)
        self.dma(Dd, d["s5dd"][l], key="s5dd")
        maskF = self.cst[:, C_MF:C_MF + 128].unsqueeze(1).to_broadcast([128, 4, 128])
        maskB = self.cst[:, C_MB:C_MB + 128].unsqueeze(1).to_broadcast([128, 4, 128])
        Q3 = [q.rearrange("p (g m) -> p g m", g=32) for q in Q]
        CF3 = [c.rearrange("p (g m) -> p g m", g=32) for c in CF]
        Gb3 = Gb.rearrange("p (g m) -> p g m", g=32)
        for g4 in range(8):
            psF = self.bank(); psB = self.bank()
            for k in range(4):
                g = g4 * 4 + k
                self.mm(psF[:, k * 128:(k + 1) * 128], Q3[0][:, g, :], CF3[0][:, g, :])
                self.mm(psB[:, k * 128:(k + 1) * 128], Q3[1][:, g, :], CF3[1][:, g, :])
            f4 = lambda x: x.rearrange("p (a m) -> p a m", a=4)
            self.tt(f4(Gs), f4(psF), maskF, ALU.mult)
            self.tt(f4(Gs2), f4(psB), maskB, ALU.mult)
            self.tt(Gs, Gs, Gs2, ALU.add)
            for k in range(4):
                g = g4 * 4 + k
                self.stt(Gb3[:, g, :], identf, Dd[:, g:g + 1], Gs[:, k * 128:(k + 1) * 128], ALU.mult, ALU.add)
        self.dma(d["s5G_d"][l].rearrange("g p m -> p g m"), Gb3, key="s5st")

    def ln_stats(self, x, slot, eps=EPS):
        base = slot * 16
        st = self.small[:, base:base + 12].rearrange("p (a b) -> p a b", a=2)
        mv = self.small[:, base + 12:base + 14]
        rstd = self.small[:, base + 14:base + 15]
        nmr = self.small[:, base + 15:base + 16]
        for h in range(2):
            self.P.add("dve", lambda e, o=st[:, h, :], i=x[:, h * 512:(h + 1) * 512]: e.bn_stats(o, i),
                       reads=[x[:, h * 512:(h + 1) * 512]], writes=[st[:, h, :]])
        self.P.add("dve", lambda e, o=mv, i=st: e.bn_aggr(o, i), reads=[st], writes=[mv])
        self.act(rstd, mv[:, 1:2], AF.Sqrt, bias=eps)
        self.P.add("dve", lambda e: e.reciprocal(rstd, rstd), reads=[rstd], writes=[rstd])
        self.stt(nmr, mv[:, 0:1], -1.0, rstd, ALU.mult, ALU.mult)
        return rstd, nmr

    def ln_mod_T(self, s, l, tiles, hT, col0, mod_shift, mod_scale, A0, src_name="xs_d"):
        d = self.dram
        xt = [self.av(A0 + i * 4096, 1024) for i in range(2)]
        xn = [self.av(A0 + 8192 + i * 2048, 1024, BF16) for i in range(4)]
        groups = []
        cur = []
        for t in tiles:
            cond = 2 if t < 2 else s
            if cur and (len(cur) == 4 or cur[0][1] != cond):
                groups.append(cur); cur = []
            cur.append((t, cond))
        if cur:
            groups.append(cur)
        pos = 0
        n = 0
        for grp in groups:
            cond = grp[0][1]
            ng = len(grp)
            for gi, (t, _) in enumerate(grp):
                x = xt[n % 2]; xb = xn[gi]
                self.dma(x, d[src_name][s, t * 128:(t + 1) * 128, :], key=f"lnx{n % 2}")
                rstd, nmr = self.ln_stats(x, n % 4)
                self.act(xb, x, AF.Identity, bias=nmr, scale=rstd)
                n += 1
            for kt in range(8):
                pb = self.bank()
                for gi in range(ng):
                    self.mm(pb[:, gi * 128:(gi + 1) * 128], xn[gi][:, kt * 128:(kt + 1) * 128], self.identb[:])
                src = pb[:, 0:ng * 128]
                dst = hT[:, kt, col0 + pos:col0 + pos + ng * 128]
                sc = self.adaT[:, l, mod_scale * 8 + kt, cond:cond + 1]
                sh = self.adaT[:, l, mod_shift * 8 + kt, cond:cond + 1]
                if kt % 2 == 0:
                    self.act(dst, src, AF.Identity, bias=sh, scale=sc)
                else:
                    self.ts(dst, src, sc, sh, ALU.mult, ALU.add)
            pos += ng * 128

    def layer(self, s, l):
        d = self.dram
        last = (l == DEPTH - 1)
        slot = l % 2
        pv = self.pv[:, slot, :]
        self.dma(pv, d["pvec"][l], key=f"pv{slot}")
        PV_GLUB = 0; PV_LCW = 4; PV_LCB = 20; PV_LBA = 24; PV_LBX = 32; PV_LLAM = 40; PV_FCW = 48; PV_FCB = 246
        coef = self.pv2[:, slot, 0:8]
        hcf = self.pv2[:, slot, 8:16]
        hba = self.pv2[:, slot, 16:24]
        hbx = self.pv2[:, slot, 24:32]
        self.act(coef, pv[:, PV_LLAM:PV_LLAM + 8], AF.Exp, scale=-1.0)
        self.act(coef, coef, AF.Ln, bias=1.0)
        self.ts(coef, coef, -8.0, None, ALU.mult)
        self.ts(hcf, coef, 0.5, None, ALU.mult)
        self.ts(hba, pv[:, PV_LBA:PV_LBA + 8], 0.5, None, ALU.mult)
        self.ts(hbx, pv[:, PV_LBX:PV_LBX + 8], 0.5, None, ALU.mult)

        H1T = 0
        YALL = 36864
        UALL = 73728
        TMPB = 92160
        h1T = self.av(H1T, 8 * NT, BF16).rearrange("p (k t) -> p k t", k=8)
        yall = self.av(YALL, 8 * NT, BF16).rearrange("p (k t) -> p k t", k=8)
        uall = self.av(UALL, 32 * NCH, BF16).rearrange("p (g m) -> p g m", g=32)

        self.ln_mod_T(s, l, list(range(TT)), h1T, 0, mod_shift=0, mod_scale=1, A0=TMPB,
                      src_name=("xin" if l == 0 else "xs_d"))
        if l == 0 and s == 0:
            self.tap("h1T", h1T.rearrange("p k t -> p (k t)"))

        if self.limit < 4:
            return
        chunks = [(0, 512), (512, 512), (1024, 512), (1536, 512), (2048, 256)]

        def load_win(ft, slot_i):
            w = self.av(TMPB + 16384 + slot_i * 2048, 1024, BF16)
            self.dma(w, d["w_in_r"][l, ft], key=f"win{slot_i}", q="pool")
            return w.rearrange("p (k c) -> p k c", k=8)

        ufm = [self.av(TMPB + i * 4608, NT, BF16) for i in range(2)]

        def s5proj(q):
            w_u = load_win(q, 2)
            u = ufm[q % 2]
            for ci, (c0, cn) in enumerate(chunks):
                ps = self.bank()
                for kt in range(8):
                    self.mm(ps[:, 0:cn], w_u[:, kt, :], h1T[:, kt, c0:c0 + cn], start=(kt == 0), stop=(kt == 7))
                self.cp(u[:, c0:c0 + cn], ps[:, 0:cn], eng="dve")
            for g8 in range(8):
                ps = self.bank()
                for j in range(8):
                    self.mm(ps[:, 0:NCH], self.bandb[:, g8, 112 - 16 * j:240 - 16 * j], u[:, j:NT:8],
                            start=(j == 0), stop=(j == 7))
                self.cp(uall[:, q * 8 + g8, :], ps[:, 0:NCH], eng="dve")

        B = TMPB + 16384 + 6144
        xlp = self.av(B, 2310); B += 9280
        xc = self.av(B, NT); B += 9216
        xcb = self.av(B, NT, BF16); B += 4608
        hsum = self.av(B, NT); B += 9216
        gw = self.av(B, 512, BF16); B += 1024
        afull = self.av(B, NT); B += 9216
        bfull = self.av(B, NT); B += 9216
        sfull = self.av(B, NT); B += 9216
        ctmp = [self.av(B + i * 2048, 512) for i in range(4)]; B += 4 * 2048
        gg = [self.av(B + i * 1024, 512, BF16) for i in range(2)]; B += 2048
        assert B <= self.ASZ, (B, self.ASZ)
        XC0 = 2; XL0 = 261
        for q in range(4):
            w_x = load_win(4 + q, 0)
            w_g = load_win(8 + q, 1)
            gw4 = gw.rearrange("p (a c) -> p a c", a=4)
            self.dma(gw, d["gate_r"][l, q], key="gatew", q="pool")
            self.memset(xlp[:, 0:2], 0.0)
            self.memset(xlp[:, 258:261], 0.0)
            self.memset(xlp[:, 2309:2310], 0.0)
            for ci, (c0, cn) in enumerate(chunks):
                ps = self.bank()
                for kt in range(8):
                    self.mm(ps[:, 0:cn], w_x[:, kt, :], h1T[:, kt, c0:c0 + cn], start=(kt == 0), stop=(kt == 7))
                if c0 == 0:
                    self.cp(xlp[:, XC0:XC0 + 256], ps[:, 0:256], eng="act")
                    self.cp(xlp[:, XL0:XL0 + 256], ps[:, 256:512], eng="act")
                else:
                    self.cp(xlp[:, XL0 + c0 - 256:XL0 + c0 - 256 + cn], ps[:, 0:cn], eng="act")
            s5proj(q)
            for (o0, on, i0) in ((0, 256, XC0 - 2), (256, 2048, XL0 - 2)):
                cw = lambda k: pv[:, PV_LCW + q * 4 + k:PV_LCW + q * 4 + k + 1]
                self.ts(xc[:, o0:o0 + on], xlp[:, i0:i0 + on], cw(0), pv[:, PV_LCB + q:PV_LCB + q + 1], ALU.mult, ALU.add)
                for k in range(1, 4):
                    self.stt(xc[:, o0:o0 + on], xlp[:, i0 + k:i0 + k + on], cw(k), xc[:, o0:o0 + on], ALU.mult, ALU.add)
            self.cp(xcb, xc, eng="act")
            if l == 0 and s == 0 and q == 0:
                self.tap("xc0", xc)
            for dd in range(2):
                c_hcf = hcf[:, dd * 4 + q:dd * 4 + q + 1]
                c_hba = hba[:, dd * 4 + q:dd * 4 + q + 1]
                c_hbx = hbx[:, dd * 4 + q:dd * 4 + q + 1]
                for ci, (c0, cn) in enumerate(chunks):
                    psr = self.bank(); psi = self.bank()
                    self.mm(psr[:, 0:cn], gw4[:, dd * 2 + 0, :], xcb[:, c0:c0 + cn])
                    self.mm(psi[:, 0:cn], gw4[:, dd * 2 + 1, :], xcb[:, c0:c0 + cn])
                    t1 = ctmp[(ci % 2) * 2]; t2 = ctmp[(ci % 2) * 2 + 1]
                    self.act(t1[:, 0:cn], psr[:, 0:cn], AF.Tanh, bias=c_hba, scale=0.5)
                    self.act(afull[:, c0:c0 + cn], t1[:, 0:cn], AF.Exp, bias=c_hcf, scale=c_hcf)
                    self.act(t2[:, 0:cn], psi[:, 0:cn], AF.Tanh, bias=c_hbx, scale=0.5)
                    self.stt(bfull[:, c0:c0 + cn], t2[:, 0:cn], 1.0, xc[:, c0:c0 + cn], ALU.add, ALU.mult)
                self.act(sfull, afull, AF.Square)
                self.act(sfull, sfull, AF.Sqrt, bias=1.0, scale=-1.0)
                self.stt(bfull, sfull, 0.5, bfull, ALU.mult, ALU.mult)

                def scan(o, a, b, init, rev):
                    rd = [a, b] + ([] if isinstance(init, float) else [init])
                    if rev:
                        self.P.add("dve", lambda e: e.tensor_tensor_scan(o[:, ::-1], a[:, ::-1], b[:, ::-1], init,
                                                                         ALU.mult, ALU.add), reads=rd, writes=[o])
                    else:
                        self.P.add("dve", lambda e: e.tensor_tensor_scan(o, a, b, init, ALU.mult, ALU.add),
                                   reads=rd, writes=[o])

                if dd == 0:
                    scan(hsum, afull, bfull, 0.0, False)
                else:
                    scan(sfull[:, 0:256], afull[:, 0:256], bfull[:, 0:256], 0.0, True)
                    scan(sfull[:, 256:NT], afull[:, 256:NT], bfull[:, 256:NT], sfull[:, 0:1], True)
                    self.tt(hsum, hsum, sfull, ALU.add)
            for ci, (c0, cn) in enumerate(chunks):
                ps = self.bank()
                for kt in range(8):
                    self.mm(ps[:, 0:cn], w_g[:, kt, :], h1T[:, kt, c0:c0 + cn], start=(kt == 0), stop=(kt == 7))
                g = gg[ci % 2]
                self.act(g[:, 0:cn], ps[:, 0:cn], AF.Gelu_apprx_tanh)
                self.tt(yall[:, 4 + q, c0:c0 + cn], hsum[:, c0:c0 + cn], g[:, 0:cn], ALU.mult)
            if l == 0 and s == 0 and q == 0:
                self.tap("hsum0", hsum)

        if self.limit < 5:
            return
        if self.limit < 6:
            return
        YG = 0
        RING = 18432
        ZS = TMPB
        yg = self.av(YG, 4 * NT, BF16).rearrange("p (k t) -> p k t", k=4)
        zs = self.av(ZS, NCH * 64).rearrange("p (i c) -> p i c", c=64)
        ZE = ZS + NCH * 64 * 4
        ys = [self.av(ZE + i * 4608, 8 * NCH, BF16).rearrange("p (g m) -> p g m", g=8) for i in range(1)]
        assert ZE + 4608 <= self.ASZ
        m1 = self.m12[:, 0, :]; m2 = self.m12[:, 1, :]
        if s == 0 or True:
            mu = d["s5mu_d"][l]
            for gp in range(2):
                for dd in range(2):
                    for ri in range(2):
                        c0 = ri * 32 + dd * 16
                        src_re = mu[dd, 0][:, gp:32:2]
                        src_im = mu[dd, 1][:, gp:32:2]
                        self.dma(m1[gp * 64:(gp + 1) * 64, c0:c0 + 16], src_re, key="mu", slow=True)
                        self.dma(m2[gp * 64:(gp + 1) * 64, c0:c0 + 16], src_im, key="mu", slow=True)
            self.ts(m2[:, 32:64], m2[:, 32:64], -1.0, None, ALU.mult)
        RSZ = 4608
        for g2 in range(16):
            rs = RING + (g2 % 3) * RSZ
            gt = self.av(rs, 256, BF16).rearrange("p (a m) -> p a m", a=2)
            wt = self.av(rs + 512, 512, BF16).rearrange("p (a b m) -> p a b m", a=2, b=2)
            self.dma(wt[:, :, 0, :], d["s5Win_d"][l, 0, 2 * g2:2 * g2 + 2].rearrange("g p m -> p g m"), key=f"s5w{g2 % 3}")
            self.dma(wt[:, :, 1, :], d["s5Win_d"][l, 1, 2 * g2:2 * g2 + 2].rearrange("g p m -> p g m"), key=f"s5w{g2 % 3}")
            for dd in range(2):
                for ri in range(2):
                    ps = self.bank()
                    for gp in range(2):
                        g = 2 * g2 + gp
                        o = ps[gp * 64:(gp + 1) * 64, :]
                        lhs = wt[:, gp, dd, ri * 64:(ri + 1) * 64]
                        if dd == 0:
                            self.mm(o[:, 0:NCH], lhs, uall[:, g, :])
                        else:
                            self.mm(o[:, 0:32], lhs, uall[:, g, 31::-1])
                            self.mm(o[:, 32:NCH], lhs, uall[:, g, NCH - 1:31:-1])
                    col = ri * 32 + dd * 16 + g2
                    self.cp(zs[:, :, col], ps[:, 0:NCH], eng=("act" if (dd + ri) % 2 else "dve"))
        mcat = self.m12[:]
        m2c = self.av(ZE + 4608, 128).rearrange("p (a c) -> p a c", a=2)
        tq = self.av(ZE + 4608 + 512, 128)
        self.tt(tq[:, 0:64], mcat[:, 0, :], mcat[:, 0, :], ALU.mult)
        self.tt(tq[:, 64:128], mcat[:, 1, :], mcat[:, 1, :], ALU.mult)
        self.tt(m2c[:, 0, :], tq[:, 0:64], tq[:, 64:128], ALU.subtract)
        self.tt(tq[:, 0:64], mcat[:, 0, :], mcat[:, 1, :], ALU.mult)
        self.ts(m2c[:, 1, :], tq[:, 0:64], 2.0, None, ALU.mult)
        PB = ZE + 4608 + 1024
        NBK = 24
        pblk = self.av(PB, NBK * 128).rearrange("p (i a c) -> p i a c", a=2, c=64)
        assert PB + NBK * 128 * 4 <= self.ASZ, (PB, self.ASZ)
        hi = NCH
        while hi > 1:
            lo = max(1, hi - NBK)
            n = hi - lo
            zprev = zs[:, lo - 1:hi - 1, :]
            self.tt(pblk[:, 0:n], mcat.unsqueeze(1).to_broadcast([128, n, 2, 64]),
                    zprev.unsqueeze(2).to_broadcast([128, n, 2, 64]), ALU.mult)
            plo = pblk[:, 0:n, 0, :].rearrange("p i (r c) -> p i r c", r=2)
            phi = pblk[:, 0:n, 1, :].rearrange("p i (r c) -> p i r c", r=2)[:, :, ::-1, :]
            self.tt(plo, plo, phi, ALU.add)
            self.tt(zs[:, lo:hi, :], zs[:, lo:hi, :], pblk[:, 0:n, 0, :], ALU.add)
            hi = lo
        Pt = [self.sc3[:, 0:2, :], self.av(PB, 128).rearrange("p (a c) -> p a c", a=2)]
        St = [self.sc3[:, 2, :], self.av(PB + 512, 64)]
        for i in range(2, NCH, 2):
            pair = [j for j in (i, i + 1) if j < NCH]
            for j in pair:
                k = j % 2
                xx = zs[:, j - 2, :].unsqueeze(1).to_broadcast([128, 2, 64])
                self.tt(Pt[k], m2c, xx, ALU.mult)
            for j in pair:
                k = j % 2
                self.tt(St[k].rearrange("p (r c) -> p r c", r=2), Pt[k][:, 0, :].rearrange("p (r c) -> p r c", r=2),
                        Pt[k][:, 1, :].rearrange("p (r c) -> p r c", r=2)[:, ::-1, :], ALU.add)
            for j in pair:
                k = j % 2
                self.tt(zs[:, j, :], zs[:, j, :], St[k], ALU.add)
        if l == 0 and s == 0:
            self.tap("zs", self.av(ZS, NCH * 64))
        for q in range(4):
            ysq = ys[0]
            for g8 in range(8):
                g = q * 8 + g8
                g2, gp = g // 2, g % 2
                if gp == 0:
                    rs = RING + (g2 % 3) * RSZ
                    gt = self.av(rs, 256, BF16).rearrange("p (a m) -> p a m", a=2)
                    mo = self.av(rs + 512, 512).rearrange("p (a b m) -> p a b m", a=2, b=2)
                    self.dma(gt, d["s5G_d"][l, 2 * g2:2 * g2 + 2].rearrange("g p m -> p g m"), key=f"s5y{g2 % 3}")
                    for pp in range(2):
                        for dd in range(2):
                            for ri in range(2):
                                self.dma(mo[pp * 64:(pp + 1) * 64, dd, ri, :], d["s5Mout_d"][l, dd, ri][:, 2 * g2 + pp, :],
                                         key=f"s5y{g2 % 3}")
                ps = self.bank()
                pr = slice(gp * 64, (gp + 1) * 64)
                self.mm(ps[:, 0:NCH], gt[:, gp, :], uall[:, g, :], start=True, stop=False)
                for ri in range(2):
                    self.mm(ps[:, 1:NCH], mo[pr, 0, ri, :], zs[pr, 0:NCH - 1, ri * 32 + g2], start=False, stop=False)
                for ri in range(2):
                    col = ri * 32 + 16 + g2
                    self.mm(ps[:, 30::-1], mo[pr, 1, ri, :], zs[pr, 0:31, col], start=False, stop=False)
                    self.mm(ps[:, NCH - 1:31:-1], mo[pr, 1, ri, :], zs[pr, 31:NCH - 1, col], start=False, stop=(ri == 1))
                self.cp(ysq[:, g8, :], ps[:, 0:NCH], eng=("act" if g8 % 2 else "dve"))
            for pc in range(5):
                ncw = 64 if pc < 4 else 32
                ps = self.bank()
                for j in range(8):
                    for g8 in range(8):
                        self.mm(ps[:, j:ncw * 8:8], self.bandb[:, j, 112 - 16 * g8:240 - 16 * g8],
                                ysq[:, g8, pc * 64:pc * 64 + ncw], start=(g8 == 0), stop=(g8 == 7))
                self.act(yg[:, q, pc * 512:pc * 512 + ncw * 8], ps[:, 0:ncw * 8], AF.Gelu_apprx_tanh)

        if self.limit < 7:
            return
        if l == 0 and s == 0:
            self.tap("yg", yg.rearrange("p k t -> p (k t)"))
        B = UALL
        wglu = self.av(B, 4 * 4 * 128, BF16).rearrange("p (f k c) -> p f k c", f=4, k=4); B += 4096
        for ft in range(4):
            self.dma(wglu[:, ft].rearrange("p k c -> p (k c)"), d["glu_r"][l, ft], key="wglu", q="pool")
        sg = [self.av(B + i * 1024, 512, BF16) for i in range(2)]; B += 2048
        n = 0
        for ft in range(4):
            for ci, (c0, cn) in enumerate(chunks):
                ps = self.bank()
                for kt in range(4):
                    self.mm(ps[:, 0:cn], wglu[:, ft, kt, :], yg[:, kt, c0:c0 + cn], start=(kt == 0), stop=(kt == 3))
                sgt = sg[n % 2]; n += 1
                self.act(sgt[:, 0:cn], ps[:, 0:cn], AF.Sigmoid, bias=pv[:, PV_GLUB + ft:PV_GLUB + ft + 1])
                self.tt(yall[:, ft, c0:c0 + cn], yg[:, ft, c0:c0 + cn], sgt[:, 0:cn], ALU.mult)
        if l == 0 and s == 0:
            self.tap("yall", yall.rearrange("p k t -> p (k t)"))
        wo = self.av(B, 8 * D, BF16).rearrange("p (k n) -> p k n", k=8); B += 16384
        for kt in range(8):
            self.dma(wo[:, kt, :], d["w_out_r"][l, kt], key="wo", q="pool")
        tiles = list(range(2, TT)) if last else list(range(TT))
        WD = (self.ASZ - NFT * D * 2) // 64 * 64
        wd = self.av(WD, NFT * D, BF16).rearrange("p (f n) -> p f n", f=NFT)
        for ft in range(NFT):
            self.dma(wd[:, ft, :], d["w_down_r"][l, ft], key="wd", q="pool")
        B = self.resid_ln(s, l, tiles, B, mod_gate=2, lnrow0=0, to_out=False, src_name=("xin" if l == 0 else "xs_d"), lim=WD,
                          mmfn=lambda ps2, t: [self.mm(ps2[:, h * 512:(h + 1) * 512], yall[:, kt, t * 128:(t + 1) * 128],
                                                       wo[:, kt, h * 512:(h + 1) * 512], start=(kt == 0), stop=(kt == 7))
                                               for h in range(2) for kt in range(8)])

        if self.limit < 8:
            return
        self.ffn(s, l, wd, WD)

    def resid_ln(self, s, l, tiles, B, mod_gate, lnrow0, to_out, mmfn, tile_off=0, src_name="xs_d", lim=None):
        d = self.dram
        gc = self.av(B, D); B += 4096
        gx = self.av(B, D); B += 4096
        lg = self.av(B, D); B += 4096
        lb = self.av(B, D); B += 4096
        xt = [self.av(B + i * 4096, D) for i in range(3)]; B += 12288
        vt = [self.av(B + i * 4096, D) for i in range(3)]; B += 12288
        assert B <= (self.ASZ if lim is None else lim), (B, self.ASZ, lim)
        ada = d["ada_d"]
        bc = lambda ap: ap.partition_broadcast(128).rearrange("p a n -> p (a n)")
        self.dma(gc, bc(ada[l, 2:3, mod_gate * D:(mod_gate + 1) * D]), key="bc0")
        self.dma(gx, bc(ada[l, s:s + 1, mod_gate * D:(mod_gate + 1) * D]), key="bc1")
        self.dma(lg, bc(d["lnrow"][l, lnrow0:lnrow0 + 1, :]), key="bc2")
        self.dma(lb, bc(d["lnrow"][l, lnrow0 + 1:lnrow0 + 2, :]), key="bc3")
        self.ts(gc, gc, 1.0 / ALPHA, None, ALU.mult)
        self.ts(gx, gx, 1.0 / ALPHA, None, ALU.mult)

        def finish(n, t, x, v):
            self.tt(v, v, lg, ALU.mult)
            self.tt(x, v, lb, ALU.add, eng="pool")
            if to_out:
                if t >= 2:
                    self.dma(d["out"][s, (t - 2) * 128:(t - 1) * 128, :], x, key=f"wx{n % 3}", q="pool")
            else:
                self.dma(d["xs_d"][s, t * 128:(t + 1) * 128, :], x, key=f"wx{n % 3}", q="pool")

        pending = None
        for n, t in enumerate(tiles):
            ps2 = self.bank(2)
            mmfn(ps2, t - tile_off)
            x = xt[n % 3]; v = vt[n % 3]
            self.dma(x, d[src_name][s, t * 128:(t + 1) * 128, :], key=f"rx{n % 3}")
            g = gc if t < 2 else gx
            self.tt(v, ps2, g, ALU.mult)
            self.tt(v, v, x, ALU.add)
            rstd, nmr = self.ln_stats(v, n % 4, eps=EPS / (ALPHA * ALPHA))
            self.act(v, v, AF.Identity, bias=nmr, scale=rstd)
            if pending is not None:
                finish(*pending)
            pending = (n, t, x, v)
        if pending is not None:
            finish(*pending)
        return B

    def ffn(self, s, l, wd, WD):
        d = self.dram
        last = (l == DEPTH - 1)
        slot = l % 2
        pv = self.pv[:, slot, :]
        PV_FCW = 48; PV_FCB = 246
        for ch in range(2):
            if ch == 0:
                ltiles = list(range(0, 10)); own = list(range(0, 9)); r0, r1 = 0, 14
                if last:
                    own = list(range(2, 9)); ltiles = list(range(2, 10))
            else:
                ltiles = list(range(8, 18)); own = list(range(9, 18)); r0, r1 = 14, 32
            t0 = ltiles[0]
            NTK = 1280
            B = 0
            h2T = self.av(B, 8 * NTK, BF16).rearrange("p (k t) -> p k t", k=8); B += 8 * NTK * 2
            hid = self.av(B, NFT * 1152, BF16).rearrange("p (f t) -> p f t", f=NFT); B += NFT * 1152 * 2
            wu = [self.av(B + i * 4096, 2048, BF16).rearrange("p (k c) -> p k c", k=8) for i in range(3)]; B += 12288
            R = r1 - r0
            GP = 66
            ug = self.av(B, (R + 2) * GP, BF16); B += ((R + 2) * GP * 2 + 63) // 64 * 64
            uc = self.av(B, 258, BF16); B += 576
            dgs = [self.av(B + i * 2304, 9 * 128, BF16).rearrange("p (k m) -> p k m", k=9) for i in range(2)]; B += 4608
            gl = self.av(B, 1152, BF16); B += 2304
            TB = B
            self.ln_mod_T(s, l, ltiles, h2T, 0, mod_shift=3, mod_scale=4, A0=TB)
            ug3 = ug.rearrange("p (r c) -> p r c", c=GP)
            self.memset(ug, 0.0)
            self.memset(uc, 0.0)
            loc = lambda r: 256 + 64 * r - t0 * 128
            ra = max(r0 - 1, 0); rb = min(r1 + 1, 32)
            nown = len(own) * 128
            own0 = own[0] * 128 - t0 * 128
            has_ctx = (ch == 0 and not last)
            lo = 256 if has_ctx else 0
            for ft in range(NFT):
                w = wu[ft % 3]
                self.dma(w.rearrange("p k c -> p (k c)"), d["w_up_r"][l, ft], key=f"wu{ft % 3}", q="pool")
                cw = lambda k: pv[:, PV_FCW + ft * 9 + k:PV_FCW + ft * 9 + k + 1]
                cb = pv[:, PV_FCB + ft:PV_FCB + ft + 1]
                dg = dgs[ft % 2]
                for k in range(9):
                    self.ts(dg[:, k, :], self.identb[:], cw(k), None, ALU.mult)
                r = ra
                while r < rb:
                    nr = min(8, rb - r)
                    ps = self.bank()
                    for kt in range(8):
                        self.mm(ps[:, 0:nr * 64], w[:, kt, 0:128], h2T[:, kt, loc(r):loc(r) + nr * 64],
                                start=(kt == 0), stop=(kt == 7))
                    gr = r - (r0 - 1)
                    self.cp(ug3[:, gr:gr + nr, 1:65], ps[:, 0:nr * 64].rearrange("p (r c) -> p r c", c=64), eng="act")
                    r += nr
                rr = 0
                while rr < R:
                    nr = min(8, R - rr)
                    ps = self.bank()
                    po = ps[:, 0:nr * 64].rearrange("p (r c) -> p r c", c=64)
                    for k in range(9):
                        di, dj = k // 3, k % 3
                        self.mm(po, dg[:, k, :], ug3[:, rr + di:rr + di + nr, dj:dj + 64], start=(k == 0), stop=(k == 8))
                    self.act(gl[:, lo + rr * 64:lo + (rr + nr) * 64], ps[:, 0:nr * 64], AF.Gelu_apprx_tanh, bias=cb)
                    rr += nr
                if has_ctx:
                    ps = self.bank()
                    for kt in range(8):
                        self.mm(ps[:, 0:256], w[:, kt, 0:128], h2T[:, kt, 0:256], start=(kt == 0), stop=(kt == 7))
                    self.cp(uc[:, 1:257], ps[:, 0:256], eng="act")
                    ps = self.bank()
                    for k in range(3):
                        self.mm(ps[:, 0:256], dg[:, 3 + k, :], uc[:, k:k + 256], start=(k == 0), stop=(k == 2))
                    self.act(gl[:, 0:256], ps[:, 0:256], AF.Gelu_apprx_tanh, bias=cb)
                c = 0
                while c < nown:
                    cn = min(512, nown - c)
                    ps = self.bank()
                    for kt in range(8):
                        self.mm(ps[:, 0:cn], w[:, kt, 128:256], h2T[:, kt, own0 + c:own0 + c + cn],
                                start=(kt == 0), stop=(kt == 7))
                    self.tt(hid[:, ft, c:c + cn], ps[:, 0:cn], gl[:, c:c + cn], ALU.mult)
                    c += cn
                if l == 0 and s == 0 and ch == 0 and ft == 0:
                    self.tap("hid0", hid[:, 0, :])
            self.resid_ln(s, l, own, TB, mod_gate=5, lnrow0=2, to_out=last, tile_off=own[0], lim=WD,
                          mmfn=lambda ps2, ti: [self.mm(ps2[:, h * 512:(h + 1) * 512], hid[:, ft, ti * 128:(ti + 1) * 128],
                                                        wd[:, ft, h * 512:(h + 1) * 512], start=(ft == 0), stop=(ft == NFT - 1))
                                                for h in range(2) for ft in range(NFT)])


def make_consts():
    c = np.zeros((128, NCONST), np.float32)
    c[:, C_ID:C_ID + 128] = np.eye(128, dtype=np.float32)
    k = np.arange(128)
    s_of = k // 16
    c[:, C_MF:C_MF + 128] = (s_of[None, :] >= s_of[:, None]).astype(np.float32)
    c[:, C_MB:C_MB + 128] = (s_of[:, None] >= s_of[None, :]).astype(np.float32)
    for g8 in range(8):
        band = np.zeros((128, 240), np.float32)
        for kk in range(16 * g8, 16 * g8 + 16):
            band[kk, kk - 16 * g8 + 112] = 1.0
        c[:, C_BAND + g8 * 240:C_BAND + (g8 + 1) * 240] = band
    c[:, C_EV:C_EV + 16] = np.arange(-7, 9, dtype=np.float32)[None, :]
    c[0:64, C_SPM] = 1.0; c[64:128, C_SPM] = -1.0
    c[0:64, C_SMP] = -1.0; c[64:128, C_SMP] = 1.0
    return c


def pack_shared(inp, L=DEPTH):
    f = lambda a: np.ascontiguousarray(np.asarray(a, dtype=np.float32)[:L])
    sh = {}
    sh["w_mod"] = f(inp["w_mod"])
    sh["b_mod"] = f(inp["b_mod"])
    w_in = f(inp["w_in"])
    sh["w_in_r"] = f(w_in.reshape(L, 8, 128, 12, 128).transpose(0, 3, 2, 1, 4).reshape(L, 12, 128, 1024))
    lam_re = f(inp["s5_lam_re"]); lam_im = f(inp["s5_lam_im"])
    b_re = f(inp["s5_b_re"]); b_im = f(inp["s5_b_im"]); c_re = f(inp["s5_c_re"]); c_im = f(inp["s5_c_im"])
    pack = np.zeros((L, 2, 128, 2112), np.float32)
    lrT = lam_re.transpose(0, 1, 3, 2)
    liT = lam_im.transpose(0, 1, 3, 2)
    pack[:, :, 0:64, 0:32] = lrT; pack[:, :, 64:128, 0:32] = lrT
    pack[:, :, 0:64, 32:64] = liT; pack[:, :, 64:128, 32:64] = liT
    brT = b_re.transpose(0, 1, 3, 2, 4).reshape(L, 2, 64, 512)
    biT = b_im.transpose(0, 1, 3, 2, 4).reshape(L, 2, 64, 512)
    crT = c_re.transpose(0, 1, 4, 2, 3).reshape(L, 2, 64, 512)
    ciT = c_im.transpose(0, 1, 4, 2, 3).reshape(L, 2, 64, 512)
    pack[:, :, 0:64, 64:576] = brT; pack[:, :, 64:128, 64:576] = biT
    pack[:, :, 0:64, 576:1088] = biT; pack[:, :, 64:128, 576:1088] = brT
    pack[:, :, 0:64, 1088:1600] = crT; pack[:, :, 64:128, 1088:1600] = ciT
    pack[:, :, 0:64, 1600:2112] = ciT; pack[:, :, 64:128, 1600:2112] = crT
    sh["s5pack"] = pack
    dd = f(inp["s5_d"]).reshape(L, 32, 16).transpose(0, 2, 1)
    sh["s5dd"] = f(np.tile(dd, (1, 8, 1)))
    sh["log_dt"] = f(inp["s5_log_dt"])
    glu = f(inp["s5_w_glu"])
    sh["glu_r"] = f(glu.reshape(L, 4, 128, 4, 128).transpose(0, 3, 2, 1, 4).reshape(L, 4, 128, 512))
    wa = f(inp["lru_w_a"]); wx = f(inp["lru_w_x"])
    gate = np.zeros((L, 4, 128, 2, 2, 128), np.float32)
    for q in range(4):
        for hh in range(2):
            h = 2 * q + hh
            gate[:, q, hh * 64:(hh + 1) * 64, :, 0, hh * 64:(hh + 1) * 64] = wa[:, :, h].transpose(0, 2, 1, 3)
            gate[:, q, hh * 64:(hh + 1) * 64, :, 1, hh * 64:(hh + 1) * 64] = wx[:, :, h].transpose(0, 2, 1, 3)
    sh["gate_r"] = f(gate.reshape(L, 4, 128, 512))
    sh["w_out_r"] = f(f(inp["w_out"]).reshape(L, 8, 128, D))
    wup = f(inp["ffn_w_up"])
    u = wup[:, :, :DFF].reshape(L, 8, 128, NFT, 128)
    v = wup[:, :, DFF:].reshape(L, 8, 128, NFT, 128)
    uv = np.concatenate([u, v], axis=-1)
    sh["w_up_r"] = f(uv.transpose(0, 3, 2, 1, 4).reshape(L, NFT, 128, 2048))
    sh["w_down_r"] = f(f(inp["ffn_w_down"]).reshape(L, NFT, 128, D))
    pvec = np.zeros((L, 128, NPV), np.float32)
    tp = lambda a, n: f(a).reshape(L, n, 128).transpose(0, 2, 1)
    pvec[:, :, 0:4] = tp(inp["s5_b_glu"], 4)
    cw = f(inp["lru_conv_w"]).reshape(L, 4, 4, 128)
    pvec[:, :, 4:20] = cw.transpose(0, 3, 2, 1).reshape(L, 128, 16)
    pvec[:, :, 20:24] = tp(inp["lru_conv_b"], 4)
    pvec[:, :, 24:32] = f(inp["lru_b_a"]).reshape(L, 2, 4, 128).transpose(0, 3, 1, 2).reshape(L, 128, 8)
    pvec[:, :, 32:40] = f(inp["lru_b_x"]).reshape(L, 2, 4, 128).transpose(0, 3, 1, 2).reshape(L, 128, 8)
    pvec[:, :, 40:48] = f(inp["lru_lam"]).reshape(L, 2, 4, 128).transpose(0, 3, 1, 2).reshape(L, 128, 8)
    fcw = f(inp["ffn_conv_w"]).reshape(L, 9, NFT, 128)
    pvec[:, :, 48:246] = fcw.transpose(0, 3, 2, 1).reshape(L, 128, NFT * 9)
    pvec[:, :, 246:268] = tp(inp["ffn_conv_b"], NFT)
    sh["pvec"] = pvec
    sh["lnrow"] = f(np.stack([f(inp["ln1_g"]), f(inp["ln1_b"]), f(inp["ln2_g"]), f(inp["ln2_b"])], axis=1))
    sh["consts"] = make_consts()
    return sh


def pack_core(inp, core):
    f = lambda a: np.ascontiguousarray(np.asarray(a, dtype=np.float32))
    b0 = 2 * core
    x = f(inp["x"][b0:b0 + 2]); ctx = f(inp["ctx"][b0:b0 + 2])
    xin = np.concatenate([ctx, x], axis=1)
    conds = np.stack([f(inp["c"][b0]), f(inp["c"][b0 + 1]), f(inp["c_ctx"])], axis=0)
    condT = conds.reshape(3, 8, 128).transpose(2, 1, 0).reshape(128, 24)
    return {"xin": f(xin), "condT": f(condT)}


_CACHE = {}


def kernel(**inputs):
    if "nc" not in _CACHE:
        _CACHE["nc"] = Builder().build()
    nc = _CACHE["nc"]
    sh = pack_shared(inputs)
    in_maps = []
    for c in range(NCORES):
        m = dict(sh)
        m.update(pack_core(inputs, c))
        in_maps.append(m)
    res = run_bass_kernel_spmd(nc, in_maps, core_ids=list(range(NCORES)))
    outs = [np.asarray(r["out"], dtype=np.float32) for r in res.results]
    return np.concatenate(outs, axis=0).reshape(16, 2048, D)
```
